# Optimizing a Trainium2 kernel written in Bass

```python
import math
import jax, jax.numpy as jnp
from jax import lax
import numpy as np

D_MODEL = 1024
BATCH = 16
SEQ = 4096
DEPTH = 1
DEC_BATCH = 16
DEC_SEQ = 2048
PAST_LEN = 128

DA_HEADS = 8
DA_HEAD_DIM = 64
DA_V_DIM = 2 * DA_HEAD_DIM
DA_QK = DA_HEADS * 2 * DA_HEAD_DIM
DA_V = DA_HEADS * DA_V_DIM
Q_BLOCK = 128
DL_PAIRS = ((128, 1), (512, 4), (2048, 16))
N_GROUPS = len(DL_PAIRS)
DL_HEADS = 8
DL_HEAD_DIM = 64
DL_W = DL_HEADS * DL_HEAD_DIM
DL_BLOCK = 64
ROPE_THETA = 500000.0
ROPE_FRACTION = 4
FFN_HIDDEN = -(-8 * D_MODEL // (3 * 256)) * 256
EPS = 1e-6
NEG = -1e30
N_IN = 2 * DA_QK + DA_V + 3 * N_GROUPS * DL_W + 2 * D_MODEL

kernel_name = "hybrid_diff_dilated_encoder"


def _rmsnorm(x, g):
    xf = x.astype(jnp.float32)
    y = xf * lax.rsqrt(jnp.mean(xf * xf, axis=-1, keepdims=True) + EPS)
    return (y * g.astype(jnp.float32)).astype(x.dtype)


def _rope_tables(seq, head_dim):
    rot = head_dim // ROPE_FRACTION
    inv = 1.0 / (ROPE_THETA ** (jnp.arange(0, rot, 2, dtype=jnp.float32) / rot))
    ang = jnp.arange(seq, dtype=jnp.float32)[:, None] * inv[None, :]
    return jnp.cos(ang), jnp.sin(ang)


def _rope(t, cos, sin):
    half = cos.shape[-1]
    tf = t.astype(jnp.float32)
    t1, t2, tp = tf[..., :half], tf[..., half:2 * half], tf[..., 2 * half:]
    c = cos[None, :, None, :]
    s = sin[None, :, None, :]
    return jnp.concatenate([t1 * c - t2 * s, t2 * c + t1 * s, tp], axis=-1).astype(t.dtype)


def _diff_attention(q, k, v, lam):
    B, S, H, _, E = q.shape
    nq = S // Q_BLOCK
    qb = q.reshape(B, nq, Q_BLOCK, H, 2, E).transpose(1, 0, 2, 3, 4, 5)
    scale = E ** -0.5

    def one_block(qblk):
        s = jnp.einsum('bqhce,bkhce->bhcqk', qblk, k).astype(jnp.float32) * scale
        p = jax.nn.softmax(s, axis=-1)
        a = p[:, :, 0] - lam * p[:, :, 1]
        return jnp.einsum('bhqk,bkhe->bqhe', a.astype(v.dtype), v)

    o = lax.map(one_block, qb)
    return o.transpose(1, 0, 2, 3, 4).reshape(B, S, H, v.shape[-1])


def _neighbour_blocks(t, blk):
    nb = t.shape[2] // blk
    pad = [(0, 0), (0, 0), (blk, blk)] + [(0, 0)] * (t.ndim - 3)
    tb = jnp.pad(t, pad).reshape(t.shape[:2] + (nb + 2, blk) + t.shape[3:])
    return jnp.concatenate([tb[:, :, :-2], tb[:, :, 1:-1], tb[:, :, 2:]], axis=3)


def _dilated_group(q, k, v, dil, radius):
    B, S, H, E = q.shape
    blk = DL_BLOCK
    span = dil * blk
    Sp = -(-S // span) * span
    T = Sp // dil
    nb = T // blk
    padw = [(0, 0), (0, Sp - S), (0, 0), (0, 0)]

    def strided(t):
        return jnp.pad(t, padw).reshape(B, T, dil, H, E).transpose(0, 2, 1, 3, 4)

    qs, ks, vs = strided(q), strided(k), strided(v)
    qb = qs.reshape(B, dil, nb, blk, H, E)
    kn = _neighbour_blocks(ks, blk)
    vn = _neighbour_blocks(vs, blk)
    valid = (jnp.arange(Sp) < S).reshape(T, dil).T[None]
    kmask = _neighbour_blocks(valid, blk)
    qi = jnp.arange(blk)[:, None]
    kj = jnp.arange(3 * blk)[None, :]
    band = jnp.abs(kj - blk - qi) <= radius
    mask = band[None, None, None, None] & kmask[:, :, :, None, None, :]

    s = jnp.einsum('brnqhe,brnkhe->brnhqk', qb, kn).astype(jnp.float32) * (E ** -0.5)
    s = jnp.where(mask, s, NEG)
    m = jnp.max(s, axis=-1, keepdims=True)
    p = jnp.exp(s - m)
    den = jnp.sum(p, axis=-1)
    o = jnp.einsum('brnhqk,brnkhe->brnqhe', (p / den[..., None]).astype(v.dtype), vn)
    lse = m[..., 0] + jnp.log(den)
    o = o.reshape(B, dil, T, H, E).transpose(0, 2, 1, 3, 4).reshape(B, Sp, H, E)[:, :S]
    lse = lse.transpose(0, 1, 2, 4, 3).reshape(B, dil, T, H).transpose(0, 2, 1, 3).reshape(B, Sp, H)[:, :S]
    return o, lse


def _mixer(h, lambda_init, w_in, lam_q1, lam_k1, lam_q2, lam_k2, g_subln, w_a, w_b, w_out, cos, sin):
    B, S, _ = h.shape
    z = h @ w_in
    sizes = [DA_QK, DA_QK, DA_V] + [DL_W] * (3 * N_GROUPS) + [D_MODEL, D_MODEL]
    parts = jnp.split(z, np.cumsum(sizes)[:-1].tolist(), axis=-1)

    qa = _rope(parts[0].reshape(B, S, DA_HEADS * 2, DA_HEAD_DIM), cos, sin).reshape(B, S, DA_HEADS, 2, DA_HEAD_DIM)
    ka = _rope(parts[1].reshape(B, S, DA_HEADS * 2, DA_HEAD_DIM), cos, sin).reshape(B, S, DA_HEADS, 2, DA_HEAD_DIM)
    va = parts[2].reshape(B, S, DA_HEADS, DA_V_DIM)
    lam = (jnp.exp(jnp.sum(lam_q1.astype(jnp.float32) * lam_k1.astype(jnp.float32)))
           - jnp.exp(jnp.sum(lam_q2.astype(jnp.float32) * lam_k2.astype(jnp.float32))) + lambda_init)
    oa = _diff_attention(qa, ka, va, lam)
    oa = (_rmsnorm(oa, g_subln) * (1.0 - lambda_init)).reshape(B, S, DA_V)

    outs, lses = [], []
    for g, (window, dil) in enumerate(DL_PAIRS):
        base = 3 + 3 * g
        qg = _rope(parts[base].reshape(B, S, DL_HEADS, DL_HEAD_DIM), cos, sin)
        kg = _rope(parts[base + 1].reshape(B, S, DL_HEADS, DL_HEAD_DIM), cos, sin)
        vg = parts[base + 2].reshape(B, S, DL_HEADS, DL_HEAD_DIM)
        o, l = _dilated_group(qg, kg, vg, dil, window // (2 * dil))
        outs.append(o)
        lses.append(l)
    wgt = jax.nn.softmax(jnp.stack(lses), axis=0)
    ob = jnp.einsum('gbsh,gbshe->bshe', wgt, jnp.stack(outs).astype(jnp.float32)).astype(h.dtype).reshape(B, S, DL_W)

    merged = jax.nn.sigmoid(parts[-2]) * (oa @ w_a) + jax.nn.sigmoid(parts[-1]) * (ob @ w_b)
    return merged @ w_out


def _trunk(x, c, w_ada, b_ada, g_mix, w_in, lam_q1, lam_k1, lam_q2, lam_k2, g_subln,
           w_a, w_b, w_out, g_ffn, w_gu, w_down, g_final):
    cos, sin = _rope_tables(x.shape[1], DL_HEAD_DIM)
    for l in range(DEPTH):
        lambda_init = 0.8 - 0.6 * math.exp(-0.3 * l)
        mod = (jax.nn.silu(c) @ w_ada[l] + b_ada[l])[:, None, :]
        sh1, sc1, gt1, sh2, sc2, gt2 = jnp.split(mod, 6, axis=-1)
        h = _rmsnorm(x, g_mix[l]) * (1 + sc1) + sh1
        x = x + gt1 * _mixer(h, lambda_init, w_in[l], lam_q1[l], lam_k1[l], lam_q2[l], lam_k2[l],
                             g_subln[l], w_a[l], w_b[l], w_out[l], cos, sin)
        h = _rmsnorm(x, g_ffn[l]) * (1 + sc2) + sh2
        gate, up = jnp.split(h @ w_gu[l], 2, axis=-1)
        x = x + gt2 * ((jax.nn.silu(gate) * up) @ w_down[l])
    return _rmsnorm(x, g_final)


def setup_inputs(seed: int = 0) -> dict:
    key = jax.random.key(seed)
    ks = jax.random.split(key, 24)
    f32 = jnp.float32

    def nrm(k, shape, scale):
        return jax.random.normal(k, shape, f32) * scale

    def gain(k, shape):
        return 1.0 + 0.02 * jax.random.normal(k, shape, f32)

    return {
        'x_prompt': nrm(ks[0], (BATCH, SEQ, D_MODEL), 1.0),
        'x_sample': nrm(ks[1], (DEC_BATCH, DEC_SEQ, D_MODEL), 1.0),
        'c_prompt': nrm(ks[2], (BATCH, D_MODEL), 1.0),
        'c_sample': nrm(ks[3], (DEC_BATCH, D_MODEL), 1.0),
        'w_ada': nrm(ks[4], (DEPTH, D_MODEL, 6 * D_MODEL), D_MODEL ** -0.5),
        'b_ada': nrm(ks[5], (DEPTH, 6 * D_MODEL), 0.01),
        'g_mix': gain(ks[6], (DEPTH, D_MODEL)),
        'w_in': nrm(ks[7], (DEPTH, D_MODEL, N_IN), D_MODEL ** -0.5),
        'lam_q1': nrm(ks[8], (DEPTH, DA_HEAD_DIM), 0.1),
        'lam_k1': nrm(ks[9], (DEPTH, DA_HEAD_DIM), 0.1),
        'lam_q2': nrm(ks[10], (DEPTH, DA_HEAD_DIM), 0.1),
        'lam_k2': nrm(ks[11], (DEPTH, DA_HEAD_DIM), 0.1),
        'g_subln': gain(ks[12], (DEPTH, DA_V_DIM)),
        'w_a': nrm(ks[13], (DEPTH, DA_V, D_MODEL), DA_V ** -0.5),
        'w_b': nrm(ks[14], (DEPTH, DL_W, D_MODEL), DL_W ** -0.5),
        'w_out': nrm(ks[15], (DEPTH, D_MODEL, D_MODEL), D_MODEL ** -0.5),
        'g_ffn': gain(ks[16], (DEPTH, D_MODEL)),
        'w_gu': nrm(ks[17], (DEPTH, D_MODEL, 2 * FFN_HIDDEN), D_MODEL ** -0.5),
        'w_down': nrm(ks[18], (DEPTH, FFN_HIDDEN, D_MODEL), FFN_HIDDEN ** -0.5),
        'g_final': gain(ks[19], (D_MODEL,)),
    }


def reference(x_prompt, x_sample, c_prompt, c_sample, w_ada, b_ada, g_mix, w_in,
              lam_q1, lam_k1, lam_q2, lam_k2, g_subln, w_a, w_b, w_out,
              g_ffn, w_gu, w_down, g_final):
    y_prompt = _trunk(x_prompt, c_prompt, w_ada, b_ada, g_mix, w_in, lam_q1, lam_k1, lam_q2, lam_k2,
                      g_subln, w_a, w_b, w_out, g_ffn, w_gu, w_down, g_final)
    y_sample = _trunk(x_sample, c_sample, w_ada, b_ada, g_mix, w_in, lam_q1, lam_k1, lam_q2, lam_k2,
                      g_subln, w_a, w_b, w_out, g_ffn, w_gu, w_down, g_final)
    return (y_prompt, y_sample)
```

```python
import contextlib
import numpy as np
import ml_dtypes
import concourse.bass as bass
import concourse.mybir as mybir
from concourse.bass_utils import run_bass_kernel_spmd

F32 = mybir.dt.float32
BF16 = mybir.dt.bfloat16
U8 = mybir.dt.uint8
ALU = mybir.AluOpType
AF = mybir.ActivationFunctionType
AX = mybir.AxisListType

D = 1024
NIN = 9728
FF = 2816
EPS = 1e-6
ENGS = ("pe", "act", "dve", "pool", "sp")
SEM_WRAP = 20000
import os
DBG_SKIP = os.environ.get('DBG_SKIP', '')
DBG_GROUPS = [int(c) for c in os.environ.get('DBG_GROUPS', '012')]


class Op:
    __slots__ = ("eng", "fn", "deps", "is_dma", "key", "dval", "sig", "signals")

    def __init__(self, eng, fn, is_dma=False, key=None):
        self.eng = eng
        self.fn = fn
        self.deps = []
        self.is_dma = is_dma
        self.key = key
        self.dval = 0
        self.sig = 0
        self.signals = False


class Prog:
    def __init__(self, nc):
        self.nc = nc
        self.streams = {e: [] for e in ENGS}
        self.wr = {}
        self.rd = {}
        self.dma_cnt = {}
        self.last_dma = {}
        self.bar = {}
        self.same_eng_sync = {"act", "dve", "pool"}

    def _src(self, op):
        return ("dma", id(op)) if op.is_dma else op.eng

    def _add(self, op, d, raw):
        if d is op:
            return
        if (not d.is_dma) and (not op.is_dma) and d.eng == op.eng:
            if not (raw and d.eng in self.same_eng_sync):
                return
        op.deps.append(d)
        if not d.is_dma:
            d.signals = True

    def _track(self, op, reads, writes):
        pbr = [x for x in reads if isinstance(x, tuple) and x and x[0] == "pb"]
        if pbr:
            reads = [x for x in reads if x not in pbr]
            writes = list(writes) + [x for x in pbr if x not in writes]
        b = self.bar.pop(op.eng, None)
        if b:
            for d in b:
                self._add(op, d, True)
        for x in reads:
            for o in self.wr.get(x, {}).values():
                self._add(op, o, True)
        for x in writes:
            for o in self.wr.get(x, {}).values():
                self._add(op, o, False)
            for o in self.rd.get(x, {}).values():
                self._add(op, o, False)
        src = self._src(op)
        for x in writes:
            self.wr[x] = {src: op}
            self.rd[x] = {}
        for x in reads:
            if x in writes:
                continue
            self.rd.setdefault(x, {})[src] = op

    def op(self, eng, fn, reads=(), writes=()):
        o = Op(eng, fn)
        self._track(o, reads, writes)
        self.streams[eng].append(o)
        return o

    def dma(self, queue, out_ap, in_ap, key, reads=(), writes=()):
        def fn(e, out_ap=out_ap, in_ap=in_ap):
            return e.dma_start(out=out_ap, in_=in_ap)
        o = Op(queue, fn, is_dma=True, key=key)
        self.dma_cnt[key] = self.dma_cnt.get(key, 0) + 1
        o.dval = 16 * self.dma_cnt[key]
        self._track(o, reads, writes)
        self.streams[queue].append(o)
        self.last_dma[key] = o
        return o

    def barrier(self):
        lasts = []
        for e in ENGS:
            for o in reversed(self.streams[e]):
                if not o.is_dma:
                    o.signals = True
                    lasts.append(o)
                    break
        lasts += list(self.last_dma.values())
        self.bar = {e: list(lasts) for e in ENGS}
        self.wr = {}
        self.rd = {}

    def emit(self, final_wait_keys=()):
        nc = self.nc
        nsig = {}
        for e in ENGS:
            c = 0
            for o in self.streams[e]:
                if o.signals and not o.is_dma:
                    c += 1
                    o.sig = c
            nsig[e] = c
        with contextlib.ExitStack() as es:
            esems = {}
            for e in ENGS:
                n = max((nsig[e] + SEM_WRAP - 1) // SEM_WRAP, 1)
                esems[e] = [es.enter_context(nc.semaphore(f"s_{e}_{j}")) for j in range(n)]
            dsems = {}
            for k in self.dma_cnt:
                dsems[k] = es.enter_context(nc.semaphore(f"d_{len(dsems)}"))
            block = es.enter_context(nc.Block())
            streams = self.streams
            dma_cnt = self.dma_cnt

            def run(ename, eng):
                waited = {}
                for o in streams[ename]:
                    need = {}
                    for d in o.deps:
                        if d.is_dma:
                            k = ("d", d.key)
                            v = d.dval
                        else:
                            k = ("e", d.eng)
                            v = d.sig
                        if v > need.get(k, 0):
                            need[k] = v
                    for k, v in need.items():
                        if waited.get(k, 0) >= v:
                            continue
                        waited[k] = v
                        if k[0] == "d":
                            eng.wait_ge(dsems[k[1]], v)
                        else:
                            j = (v - 1) // SEM_WRAP
                            eng.wait_ge(esems[k[1]][j], (v - 1) % SEM_WRAP + 1)
                    ins = o.fn(eng)
                    if o.is_dma:
                        ins.then_inc(dsems[o.key], 16)
                    elif o.signals:
                        j = (o.sig - 1) // SEM_WRAP
                        ins.then_inc(esems[ename][j], 1)
                if ename == "sp":
                    for k in final_wait_keys:
                        eng.wait_ge(dsems[k], 16 * dma_cnt[k])

            @block.tensor
            def _(e):
                run("pe", e)

            @block.scalar
            def _(e):
                run("act", e)

            @block.vector
            def _(e):
                run("dve", e)

            @block.gpsimd
            def _(e):
                run("pool", e)

            @block.sync
            def _(e):
                run("sp", e)


class Arena:
    def __init__(self, nc, nbytes):
        self.t = nc.alloc_sbuf_tensor("arena", [128, nbytes], U8)
        self.n = nbytes
        self.off = 0

    def alloc(self, free_shape, dtype):
        esz = 2 if dtype == BF16 else 4
        n = int(np.prod(free_shape)) * esz
        n_al = (n + 63) // 64 * 64
        assert self.off + n_al <= self.n, f"arena overflow {self.off + n_al} > {self.n}"
        v = self.t[:, self.off:self.off + n].bitcast(dtype)
        self.off += n_al
        if len(free_shape) == 2:
            v = v.rearrange("p (a b) -> p a b", b=free_shape[1])
        elif len(free_shape) == 3:
            v = v.rearrange("p (a b c) -> p a b c", b=free_shape[1], c=free_shape[2])
        return v


def build_program(seq_lens, stages=99):
    nseq = len(seq_lens)
    T = sum(seq_lens)
    SMAX = max(seq_lens)
    nc = bass.Bass("TRN2", target_bir_lowering=False)

    def din(name, shape, dt=F32):
        return nc.dram_tensor(name, list(shape), dt, kind="ExternalInput").ap()

    x_d = din("x", [T, D])
    cT_d = din("cT", [128, 8 * nseq])
    wada_d = din("w_ada", [D, 6 * D])
    badaF_d = din("b_adaF", [128, 48])
    gmixF_d = din("g_mixF", [128, 8])
    gffnF_d = din("g_ffnF", [128, 8])
    win_d = din("w_in", [D, NIN])
    lam_d = [din(n, [64]) for n in ("lam_q1", "lam_k1", "lam_q2", "lam_k2")]
    gsub_d = din("g_subln", [128])
    wa_d = din("w_a", [D, D])
    wb_d = din("w_b", [512, D])
    wo_d = din("w_out", [D, D])
    wgu_d = din("w_gu", [D, 2 * FF])
    wdn_d = din("w_down", [FF, D])
    gfin_d = din("g_final", [D])
    ident_d = din("ident", [128, 128], BF16)
    identf_d = din("identf", [128, 128])
    rmat_d = din("rmatT", [128, 128], BF16)
    mask_d = din("mask4", [128, 1024], BF16)
    ropeC_d = din("ropeC", [128, 4096])
    ropeS_d = din("ropeS", [128, 4096])
    y_d = nc.dram_tensor("y", [T, D], F32, kind="ExternalOutput").ap()
    oaT_scr = nc.dram_tensor("oaT_scr", [8, 128, SMAX], BF16, kind="Internal").ap()
    obT_scr = nc.dram_tensor("obT_scr", [4, 128, SMAX], BF16, kind="Internal").ap()
    sg_scr = nc.dram_tensor("sg_scr", [16, 128, SMAX], BF16, kind="Internal").ap()
    x1_scr = nc.dram_tensor("x1_scr", [T, D], F32, kind="Internal").ap()

    P = Prog(nc)
    A = Arena(nc, 212480)
    pbank = [nc.alloc_psum_tensor(f"pb{i}", [128, 512], F32) for i in range(8)]

    def bk(i, qs=None):
        return [("pb", i)]

    def mm(out, lhsT, rhs, start, stop, reads, writes):
        P.op("pe", lambda e, o=out, l=lhsT, r=rhs, s=start, t=stop: e.matmul(o, l, r, start=s, stop=t), reads, writes)

    def tr(out, in_, idn, reads, writes):
        P.op("pe", lambda e, o=out, i=in_, d=idn: e.transpose(o, i, d), reads, writes)

    def act(out, in_, func, reads, writes, bias=None, scale=None, accum=None):
        kw = {}
        if bias is not None:
            kw["bias"] = bias
        if scale is not None:
            kw["scale"] = scale
        if accum is not None:
            kw["accum_out"] = accum
        P.op("act", lambda e, o=out, i=in_, f=func, kw=kw: e.activation(o, i, f, **kw), reads, writes)

    def tt(eng, out, in0, in1, op, reads, writes):
        P.op(eng, lambda e, o=out, a=in0, b=in1, p=op: e.tensor_tensor(o, a, b, p), reads, writes)

    def ts(eng, out, in0, s1, s2, op0, op1, reads, writes):
        if s2 is None:
            P.op(eng, lambda e, o=out, a=in0, s=s1, p=op0: e.tensor_scalar(o, a, s, None, p), reads, writes)
        else:
            P.op(eng, lambda e, o=out, a=in0, s=s1, s_=s2, p=op0, q=op1: e.tensor_scalar(o, a, s, s_, p, q), reads, writes)

    def stt(eng, out, in0, sc, in1, op0, op1, reads, writes):
        P.op(eng, lambda e, o=out, a=in0, s=sc, b=in1, p=op0, q=op1: e.scalar_tensor_tensor(o, a, s, b, p, q), reads, writes)

    def cp(eng, out, in_, reads, writes):
        P.op(eng, lambda e, o=out, i=in_: e.tensor_copy(o, i), reads, writes)

    def recip(out, in_, reads, writes):
        P.op("dve", lambda e, o=out, i=in_: e.reciprocal(o, i), reads, writes)

    def mset(eng, ap, val, writes):
        P.op(eng, lambda e, a=ap, v=val: e.memset(a, v), (), writes)

    def rsum(out, in_, reads, writes):
        P.op("dve", lambda e, o=out, i=in_: e.reduce_sum(o, i, AX.X), reads, writes)

    ident = A.alloc((128,), BF16)
    identf = A.alloc((128,), F32)
    rmatT = A.alloc((128,), BF16)
    mask4 = A.alloc((2, 4, 128), BF16)
    onesf = A.alloc((128,), F32)
    epst = A.alloc((1,), F32)
    cT = A.alloc((8, nseq), F32)
    siluT = A.alloc((8, nseq), F32)
    badaF = A.alloc((48,), F32)
    gmixF = A.alloc((8,), F32)
    gffnF = A.alloc((8,), F32)
    modF = A.alloc((48, nseq), F32)
    lamt = A.alloc((4, 64), F32)
    lams = A.alloc((8,), F32)
    neglam = A.alloc((1,), F32)
    gsubbc = A.alloc((128,), F32)
    ones_b = A.alloc((128,), BF16)
    gsubF = A.alloc((1,), F32)
    mvec = A.alloc((4, 8), F32)
    small = A.alloc((64,), F32)
    persist_mark = A.off

    def ld(dst, src, key, wname):
        P.dma("sp", dst, src, key, writes=[wname])

    ld(ident, ident_d, "c0", "ident")
    ld(identf, identf_d, "c1", "identf")
    ld(rmatT, rmat_d, "c2", "rmatT")
    ld(mask4.rearrange("p a b c -> p (a b c)"), mask_d, "c3", "mask4")
    ld(cT.rearrange("p a b -> p (a b)"), cT_d, "c4", "cT")
    ld(badaF, badaF_d, "c5", "badaF")
    ld(gmixF, gmixF_d, "c6", "gmixF")
    ld(gffnF, gffnF_d, "c7", "gffnF")
    for i in range(4):
        ld(lamt[:, i, :], lam_d[i].partition_broadcast(128), f"c8_{i}", ("lamt", i))
    ld(gsubbc, gsub_d.partition_broadcast(128), "c9", "gsubbc")
    mset("pool", onesf, 1.0, ["onesf"])
    mset("pool", ones_b, 1.0, ["ones_b"])
    ld(gsubF, gsub_d.rearrange("(p o) -> p o", o=1), "c10", "gsubF")
    ts("dve", gsubF, gsubF, 0.8, None, ALU.mult, None, ["gsubF"], ["gsubF"])
    mset("pool", epst, EPS, ["epst"])

    for j in range(2):
        tt("dve", lamt[:, 2 * j, :], lamt[:, 2 * j, :], lamt[:, 2 * j + 1, :], ALU.mult,
           [("lamt", 2 * j), ("lamt", 2 * j + 1)], [("lamt", 2 * j)])
        rsum(lams[:, j:j + 1], lamt[:, 2 * j, :], [("lamt", 2 * j)], [("lams", j)])
        act(lams[:, 2 + j:3 + j], lams[:, j:j + 1], AF.Exp, [("lams", j)], [("lams", 2 + j)])
    tt("dve", lams[:, 4:5], lams[:, 3:4], lams[:, 2:3], ALU.subtract, [("lams", 2), ("lams", 3)], [("lams", 4)])
    ts("dve", neglam, lams[:, 4:5], -0.2, None, ALU.add, None, [("lams", 4)], ["neglam"])
    ts("dve", gsubbc, gsubbc, 0.8, None, ALU.mult, None, ["gsubbc"], ["gsubbc"])
    cTf = cT.rearrange("p a b -> p (a b)")
    sTf = siluT.rearrange("p a b -> p (a b)")
    act(sTf, cTf, AF.Exp, ["cT"], ["siluT"], scale=-1.0)
    ts("dve", sTf, sTf, 1.0, None, ALU.add, None, ["siluT"], ["siluT"])
    recip(sTf, sTf, ["siluT"], ["siluT"])
    tt("dve", sTf, sTf, cTf, ALU.mult, ["siluT", "cT"], ["siluT"])

    A.off = persist_mark
    wada = [A.alloc((8, 512), F32) for _ in range(2)]
    wada_v = wada_d.rearrange("(k p) n -> p k n", p=128)
    for i in range(12):
        sl = i % 2
        P.dma("sp", wada[sl], wada_v[:, :, i * 512:(i + 1) * 512], f"wada{sl}", writes=[("wada", sl)])
        for sub in range(4):
            j = i * 4 + sub
            b = j % 2
            for k in range(8):
                mm(pbank[b][:, 0:nseq], wada[sl][:, k, sub * 128:(sub + 1) * 128], siluT[:, k, :],
                   k == 0, k == 7, [("wada", sl), "siluT"], bk(b))
            ts("dve", modF[:, j, :], pbank[b][:, 0:nseq], badaF[:, j:j + 1], None, ALU.add, None,
               bk(b) + ["badaF"], [("modF", j)])
    P.barrier()

    def bc3(v, n):
        return v.rearrange("p (k o) -> p k o", o=1).to_broadcast([128, 8, n])

    def prep_m(s, which):
        base = 8 if which == 0 else 32
        g = gmixF if which == 0 else gffnF
        stt("dve", mvec[:, 2 * which, :], modF[:, base:base + 8, s], 1.0, g, ALU.add, ALU.mult,
            [("modF", j) for j in range(base, base + 8)] + ["gmixF", "gffnF"], [("mvec", which)])

    def prep_gt(s, which, gtbc, diag, dname="diag"):
        base = 16 if which == 0 else 40
        for k in range(8):
            ts("dve", diag, identf, modF[:, base + k, s:s + 1], None, ALU.mult, None,
               ["identf", ("modF", base + k)], [dname])
            mm(pbank[k % 2][:, 0:128], onesf, diag, True, True, ["onesf", dname], bk(k % 2))
            cp("dve", gtbc[:, k * 128:(k + 1) * 128], pbank[k % 2][:, 0:128], bk(k % 2), ["gtbc"])

    def norm_A(xin, xname, hn, hnname, ssl, junk, jname="junk"):
        ss = small[:, ssl:ssl + 1]
        lnv = small[:, ssl + 1:ssl + 2]
        rstd = small[:, ssl + 2:ssl + 3]
        P.op("act", lambda e, a=ss: e.memzero(a), (), [("small", ssl)])
        act(junk, xin, AF.Square, [xname, ("small", ssl)], [jname, ("small", ssl)], accum=ss)
        act(lnv, ss, AF.Ln, [("small", ssl)], [("small", ssl + 1)], bias=epst, scale=1.0 / D)
        act(rstd, lnv, AF.Exp, [("small", ssl + 1)], [("small", ssl + 2)], scale=-0.5)
        act(hn, xin, AF.Copy, [xname, ("small", ssl + 2)], [hnname], scale=rstd)

    def norm_B(hn, hnname, tmpA, dstT, dst_names, which, s, blkcol, pb, tname="tmpA"):
        pv = pbank[pb][:].bitcast(BF16)
        for k in range(8):
            tr(pv[:, k * 128:(k + 1) * 128], hn[:, k * 128:(k + 1) * 128], ident, [hnname, "ident"], bk(pb))
        shb = 0 if which == 0 else 24
        tt("dve", tmpA, pv.rearrange("p (k n) -> p k n", n=128), bc3(mvec[:, 2 * which, :], 128), ALU.mult,
           bk(pb) + [("mvec", which)], [tname])
        tt("dve", dstT[:, :, blkcol:blkcol + 128], tmpA, bc3(modF[:, shb:shb + 8, s], 128), ALU.add,
           [tname] + [("modF", j) for j in range(shb, shb + 8)], dst_names)

    def norm_to_T(xin, xname, hn, hnname, ssl, junk, tmpA, dstT, dst_names, which, s, blkcol, pb, jname="junk", tname="tmpA"):
        norm_A(xin, xname, hn, hnname, ssl, junk, jname)
        norm_B(hn, hnname, tmpA, dstT, dst_names, which, s, blkcol, pb, tname)

    seq_off = [int(sum(seq_lens[:i])) for i in range(nseq)]

    for s in range(nseq if stages >= 1 else 0):
        S = seq_lens[s]
        nb = S // 128
        nt = S // 512
        t0 = seq_off[s]
        A.off = persist_mark
        ropeC = A.alloc((4096,), F32)
        ropeS = A.alloc((4096,), F32)
        hT = A.alloc((8, SMAX), BF16)
        ab_mark = A.off
        ld(ropeC, ropeC_d, "rc", "ropeC")
        ld(ropeS, ropeS_d, "rs", "ropeS")
        prep_m(s, 0)

        xt = [A.alloc((1024,), F32) for _ in range(3)]
        hn = [A.alloc((1024,), BF16) for _ in range(2)]
        junk = A.alloc((1024,), BF16)
        tmpA2 = [A.alloc((8, 128), F32) for _ in range(2)]
        for blk in range(nb):
            sl = blk % 3
            P.dma("sp", xt[sl], x_d[t0 + blk * 128:t0 + (blk + 1) * 128, :], f"xt{sl}", writes=[("xt", sl)])
            norm_to_T(xt[sl], ("xt", sl), hn[blk % 2], ("hn", blk % 2), 4 * (blk % 2), junk, tmpA2[blk % 2], hT,
                      [("hT", blk // 4)], 0, s, blk * 128, blk % 2, tname=("tmpA", blk % 2))

        P.barrier()
        if stages < 2:
            continue
        A.off = ab_mark
        wsl = [[A.alloc((8, 128), BF16) for _ in range(3)] for _ in range(2)]
        KT = A.alloc((SMAX,), BF16)
        QT = A.alloc((SMAX,), BF16)
        KT1 = A.alloc((SMAX,), BF16)
        KTz = [KT, KT1]
        mset("pool", KT[64:128, :], 0.0, [("KTpad", 0)])
        mset("pool", KT1[0:64, :], 0.0, [("KTpad", 1)])
        zb = [A.alloc((512,), BF16) for _ in range(2)]
        rt_off = A.off
        rtb = A.alloc((2048,), F32)
        rt1 = [rtb[:, i * 512:(i + 1) * 512] for i in range(2)]
        rt2 = [rtb[:, (2 + i) * 512:(3 + i) * 512] for i in range(2)]
        VTv = A.t[:, rt_off:rt_off + 8192].bitcast(BF16)
        rt_names = [("rt1", 0), ("rt1", 1), ("rt2", 0), ("rt2", 1)]
        PT = [A.alloc((512,), BF16) for _ in range(4)]
        stage = [A.alloc((512,), BF16) for _ in range(2)]
        Vreg_off = A.off
        Vext = A.alloc((32, 129), BF16)
        A.off = Vreg_off
        Vd = A.alloc((33, 2, 128), BF16)
        b_mark = A.off
        win_v = win_d.rearrange("(k p) n -> p k n", p=128)
        ucount = [0]

        def load_w(cols):
            u = ucount[0] % 2
            ucount[0] += 1
            for j, c0 in enumerate(cols):
                P.dma("pool", wsl[u][j], win_v[:, :, c0:c0 + 128], f"w{u}{j}", writes=[("w", u, j)])
            return u

        def hT_names(tok_lo, tok_hi):
            return [("hT", t) for t in range(tok_lo // 512, (tok_hi - 1) // 512 + 1)]

        pz = [0]

        def proj_T_rope(u, j, dst, dname, split=None, perm=1):
            bs = {}

            def stA(t):
                b = pz[0] % 2
                pz[0] += 1
                bs[t] = b
                for k in range(8):
                    mm(pbank[b][:], wsl[u][j][:, k, :], hT[:, k, t * 512:(t + 1) * 512], k == 0, k == 7,
                       [("w", u, j), ("hT", t)], bk(b))
                act(zb[b], pbank[b][:], AF.Copy, bk(b), [("zb", b)])

            def stB(t):
                b = bs.pop(t)
                rb = 6 + b
                mm(pbank[rb][:], rmatT, zb[b], True, True, ["rmatT", ("zb", b)], bk(rb))
                tt("dve", rt1[b], pbank[b][:], ropeC[:, t * 512:(t + 1) * 512], ALU.mult, bk(b) + ["ropeC"], [("rt1", b)])
                tt("dve", rt2[b], pbank[rb][:], ropeS[:, t * 512:(t + 1) * 512], ALU.mult, bk(rb) + ["ropeS"], [("rt2", b)])
                def dsl(d_ap, p0, p1):
                    if perm == 1:
                        return d_ap[p0:p1, t * 512:(t + 1) * 512]
                    w_ = 512 // perm
                    return d_ap[p0:p1, 0:S].rearrange("p (r t) -> p r t", r=perm)[:, :, t * w_:(t + 1) * w_]

                def ssl(s_ap, p0, p1):
                    if perm == 1:
                        return s_ap[p0:p1, :]
                    return s_ap[p0:p1, :].rearrange("p (a r) -> p r a", r=perm)

                if split is None:
                    tt("pool", dsl(dst, 0, 128), ssl(rt1[b], 0, 128), ssl(rt2[b], 0, 128), ALU.add, [("rt1", b), ("rt2", b)], [(dname, t)])
                else:
                    for hd_ in range(2):
                        tt("pool" if hd_ == 0 else "dve", dsl(split[hd_], hd_ * 64, (hd_ + 1) * 64),
                           ssl(rt1[b], hd_ * 64, (hd_ + 1) * 64), ssl(rt2[b], hd_ * 64, (hd_ + 1) * 64), ALU.add,
                           [("rt1", b), ("rt2", b)], [(dname, t, hd_)])

            for t in range(nt + 1):
                if t < nt:
                    stA(t)
                if t >= 1:
                    stB(t - 1)

        A.off = b_mark
        rS = [A.alloc((512,), F32) for _ in range(2)]
        tS = [A.alloc((512,), F32) for _ in range(2)]
        aS = A.alloc((512,), F32)
        sqS = A.alloc((512,), BF16)
        lnS = A.alloc((512,), F32)
        rsS = A.alloc((512,), F32)
        Vt = Vext.rearrange("p a b -> p (a b)")[:, 0:32 * 128].rearrange("p (a b) -> p a b", b=128)
        stg_n = [0]
        ptn = [0]
        da_cols = [[h * 128, 1024 + h * 128, 2048 + h * 128] for h in range(8)]
        dl_units = [(pr, g) for pr in range(4) for g in range(3)]
        dl_cols = [[3072 + g * 1536 + pr * 128, 3072 + g * 1536 + 512 + pr * 128, 3072 + g * 1536 + 1024 + pr * 128] for pr, g in dl_units]
        gate_cols = [[7680 + (gu + j) * 128 for j in range(min(3, 16 - gu))] for gu in range(0, 16, 3)]
        all_cols = da_cols + (dl_cols if stages >= 3 else []) + (gate_cols if stages >= 4 else [])
        pend = [load_w(all_cols[0])]
        ui = [0]

        def next_u():
            u = pend[0]
            ui[0] += 1
            if ui[0] < len(all_cols):
                pend[0] = load_w(all_cols[ui[0]])
            return u

        for h in range(8):
            u = next_u()
            proj_T_rope(u, 0, QT, "QT")
            proj_T_rope(u, 1, KT, "KT", split=KTz)
            for t in range(nt):
                b = pz[0] % 2
                pz[0] += 1
                for bb in range(4):
                    for k in range(8):
                        mm(pbank[b][:, bb * 128:(bb + 1) * 128], hT[:, k, (4 * t + bb) * 128:(4 * t + bb + 1) * 128],
                           wsl[u][2][:, k, :], k == 0, k == 7, [("w", u, 2), ("hT", t)], bk(b))
                cp("dve", Vt[:, 4 * t:4 * t + 4, :], pbank[b][:].rearrange("p (a c) -> p a c", c=128), bk(b),
                   [("V", 4 * t + i) for i in range(4)])
            steps = [(qt, c, kb) for qt in range(nt) for c in range(2) for kb in range(nb)]
            slots = {}

            def da_qk(i):
                qt, c, kb = steps[i]
                sb = 1 + i % 3
                mm(pbank[sb][:], KTz[c][:, kb * 128:(kb + 1) * 128], QT[:, qt * 512:(qt + 1) * 512],
                   True, True, [("KT", kb // 4, c), ("KTpad", c), ("QT", qt)], bk(sb))
                p = ptn[0] % 4
                ptn[0] += 1
                slots[i] = p
                act(PT[p], pbank[sb][:], AF.Exp, bk(sb), [("PT", p)], scale=0.125)

            def da_av(i):
                qt, c, kb = steps[i]
                p = slots.pop(i)
                mm(pbank[4 + c][:], Vt[:, kb, :], PT[p], kb == 0, kb == nb - 1, [("PT", p), ("V", kb)], bk(4 + c))
                mm(pbank[6 + c][:], ones_b, PT[p], kb == 0, kb == nb - 1, [("PT", p), "ones_b"], bk(6 + c))
                if kb == nb - 1:
                    recip(rS[c], pbank[6 + c][:], bk(6 + c), [("rS", c)])
                    tt("dve", tS[c], pbank[4 + c][:], rS[c], ALU.mult, bk(4 + c) + [("rS", c)], [("tS", c)])
                    if c == 1:
                        stt("dve", aS, tS[1], neglam, tS[0], ALU.mult, ALU.add, [("tS", 0), ("tS", 1), "neglam"], ["aS"])
                        tt("pool", sqS, aS, aS, ALU.mult, ["aS"], ["sqS"])
                        pending.append((i_now[0] + 10, qt))

            def da_norm2(qt):
                st = stage[stg_n[0] % 2]
                stn = ("stage", stg_n[0] % 2)
                stg_n[0] += 1
                mm(pbank[0][:], ones_b, sqS, True, True, ["ones_b", "sqS"], bk(0))
                act(lnS, pbank[0][:], AF.Ln, bk(0), ["lnS"], bias=epst, scale=1.0 / 128)
                act(rsS, lnS, AF.Exp, ["lnS"], ["rsS"], scale=-0.5)
                stt("dve", st, aS, gsubF, rsS, ALU.mult, ALU.mult, ["aS", "gsubF", "rsS"], [stn])
                P.dma("sp", oaT_scr[h, :, qt * 512:(qt + 1) * 512], st, f"st{stn[1]}", reads=[stn], writes=[("oaT_scr", h, qt)])

            n = len(steps)
            pending = []
            i_now = [0]
            for i in range(n + 2):
                i_now[0] = i
                if i < n:
                    da_qk(i)
                if i >= 2:
                    da_av(i - 2)
                while pending and pending[0][0] <= i:
                    da_norm2(pending.pop(0)[1])
            while pending:
                da_norm2(pending.pop(0)[1])

        if stages >= 3:
            P.barrier()
            A.off = b_mark
            acc = A.alloc((2, SMAX), F32)
            Osb = [A.alloc((256,), BF16) for _ in range(4)]
            for hd_ in range(2):
                mset("pool", Vd[:, :, hd_, 64:128], 1.0, [("V", i) for i in range(33)])
            pend_fin = []

            def emit_fin(pr_):
                for t in range(nt):
                    st = stage[stg_n[0] % 2]
                    stn = ("stage", stg_n[0] % 2)
                    stg_n[0] += 1
                    recip(acc[:, 1, t * 512:(t + 1) * 512], acc[:, 1, t * 512:(t + 1) * 512], [("acc", t)], [("acc", t)])
                    tt("dve", st, acc[:, 0, t * 512:(t + 1) * 512], acc[:, 1, t * 512:(t + 1) * 512], ALU.mult, [("acc", t)], [stn])
                    P.dma("sp", obT_scr[pr_, :, t * 512:(t + 1) * 512], st, f"st{stn[1]}", reads=[stn], writes=[("obT_scr", pr_, t)])

            for pr in range(4):
                for g, dil in enumerate((1, 4, 16)):
                    u = next_u()
                    if g not in DBG_GROUPS:
                        continue
                    proj_T_rope(u, 0, QT, "QT", perm=dil)
                    proj_T_rope(u, 1, KT, "KT", split=KTz, perm=dil)
                    while pend_fin:
                        emit_fin(pend_fin.pop(0))
                    Tn = S // dil
                    nj = Tn // 128
                    for t in range(nt):
                        b = pz[0] % 2
                        pz[0] += 1
                        for k in range(8):
                            mm(pbank[b][:], wsl[u][2][:, k, :], hT[:, k, t * 512:(t + 1) * 512], k == 0, k == 7,
                               [("w", u, 2), ("hT", t)], bk(b))
                        if dil == 1:
                            vdst = VTv[:, t * 512:(t + 1) * 512]
                            vsrc = pbank[b][:]
                        else:
                            w_ = 512 // dil
                            vdst = VTv[:, 0:S].rearrange("p (r t) -> p r t", r=dil)[:, :, t * w_:(t + 1) * w_]
                            vsrc = pbank[b][:].rearrange("p (a r) -> p r a", r=dil)
                        act(vdst, vsrc, AF.Copy, bk(b), rt_names)
                    for r in range(dil):
                        blocks = []
                        for jj in range(nj + 1):
                            lo = max(128 * jj - 64, 0)
                            hi = min(128 * jj + 64, Tn)
                            blocks.append((lo, hi - lo))
                        for g4 in range(0, nj + 1, 4):
                            b = pz[0] % 2
                            pz[0] += 1
                            pvb = pbank[b][:].bitcast(BF16)
                            grp = blocks[g4:g4 + 4]
                            for bi, (lo, nk) in enumerate(grp):
                                c0 = r * Tn + lo
                                tr(pvb[0:nk, bi * 128:(bi + 1) * 128], VTv[:, c0:c0 + nk], ident, rt_names + ["ident"], bk(b))
                            ng = len(grp)
                            for hd_ in range(2):
                                cp("dve", Vd[:, g4:g4 + ng, hd_, 0:64],
                                   pvb[:, 0:ng * 128].rearrange("p (a h c) -> p a h c", h=2, c=64)[:, :, hd_, :],
                                   bk(b), [("V", g4 + i) for i in range(ng)])
                        info = {}

                        def dl_qk(j):
                            sb = 2 + j % 2
                            q0 = r + dil * 128 * j
                            q1 = q0 + dil * 127 + 1
                            kinfo = []
                            mk = 1 if j == 0 else 0
                            mm(pbank[sb][:], ident, mask4[:, mk].rearrange("p a b -> p (a b)"), True, False, ["ident", "mask4"], bk(sb))
                            for ab in range(2):
                                lo, nk = blocks[j + ab]
                                tok0 = r + dil * lo
                                tok1 = tok0 + dil * (nk - 1) + 1
                                kinfo.append((nk, tok0, tok1))
                                for hd in range(2):
                                    qd = hd * 2 + ab
                                    if nk == 64 and 'b' in DBG_SKIP:
                                        continue
                                    ksl = slice(r * Tn + lo, r * Tn + lo + nk)
                                    qsl = slice(r * Tn + 128 * j, r * Tn + 128 * j + 128)
                                    mm(pbank[sb][0:nk, qd * 128:(qd + 1) * 128], KTz[hd][:, ksl], QT[:, qsl], False, (ab == 1 and hd == 1),
                                       [("KT", t, hd) for t in range(nt)] + [("KTpad", hd)] + [("QT", t) for t in range(nt)], bk(sb))
                            p = ptn[0] % 4
                            ptn[0] += 1
                            act(PT[p], pbank[sb][:], AF.Exp, bk(sb), [("PT", p)], scale=0.125)
                            info[j] = (p, kinfo, q0, q1)

                        def dl_av(j):
                            p, kinfo, q0, q1 = info[j]
                            ob = 4 + j % 2
                            oh = 0
                            ops_ = pbank[ob][:, 0:256]
                            for hd in range(2):
                                for ab in range(2):
                                    nk = kinfo[ab][0]
                                    qd = hd * 2 + ab
                                    mm(ops_[:, hd * 128:(hd + 1) * 128], PT[p][0:nk, qd * 128:(qd + 1) * 128], Vd[0:nk, j + ab, hd, :],
                                       ab == 0, ab == 1, [("PT", p), ("V", j + ab)], bk(ob, (2 * oh, 2 * oh + 1)))
                            cp("dve", Osb[j % 4].rearrange("p (a h c) -> p a h c", a=2, h=2),
                               ops_.rearrange("p (h a c) -> p a h c", h=2, a=2), bk(ob), [("Osb", j % 4)])

                        def dl_tr(j):
                            p, kinfo, q0, q1 = info.pop(j)
                            tb = 6 + j % 2
                            pv = pbank[tb][:].bitcast(BF16)
                            z = j % 4
                            for w_ in range(2):
                                tr(pv[:, w_ * 128:(w_ + 1) * 128], Osb[z][:, w_ * 128:(w_ + 1) * 128], ident,
                                   [("Osb", z), "ident"], bk(tb))
                            names = [("acc", t) for t in range(q0 // 512, (q1 - 1) // 512 + 1)]
                            asl = slice(q0, q1) if dil == 1 else slice(q0, q1, dil)
                            src = pv[:, 0:256].rearrange("p (a c) -> p a c", c=128)
                            if g == 0:
                                cp("dve", acc[:, :, asl], src, bk(tb), names)
                            else:
                                tt("dve", acc[:, :, asl], acc[:, :, asl], src, ALU.add, bk(tb) + names, names)

                        for i in range(nj + 3):
                            if i < nj and 'q' not in DBG_SKIP:
                                dl_qk(i)
                            if 2 <= i < nj + 2 and 'a' not in DBG_SKIP:
                                dl_av(i - 2)
                            if 3 <= i and 'a' not in DBG_SKIP and 't' not in DBG_SKIP:
                                dl_tr(i - 3)
                pend_fin.append(pr)
            while pend_fin:
                emit_fin(pend_fin.pop(0))

        if stages >= 4:
            P.barrier()
            A.off = b_mark
            et = [A.alloc((512,), F32) for _ in range(2)]
            for gu in range(0, 16, 3):
                cols = [7680 + (gu + j) * 128 for j in range(min(3, 16 - gu))]
                u = next_u()
                for j in range(len(cols)):
                    for t in range(nt):
                        b = pz[0] % 2
                        pz[0] += 1
                        for k in range(8):
                            mm(pbank[b][:], wsl[u][j][:, k, :], hT[:, k, t * 512:(t + 1) * 512], k == 0, k == 7,
                               [("w", u, j), ("hT", t)], bk(b))
                        st = stage[stg_n[0] % 2]
                        stn = ("stage", stg_n[0] % 2)
                        stg_n[0] += 1
                        act(st, pbank[b][:], AF.Sigmoid, bk(b), [stn])
                        P.dma("sp", sg_scr[gu + j, :, t * 512:(t + 1) * 512], st, f"st{stn[1]}", reads=[stn], writes=[("sg_scr", gu + j, t)])
        P.barrier()
        if stages < 5:
            continue
        A.off = persist_mark
        wa = A.alloc((8, 1024), BF16)
        wb = A.alloc((4, 1024), BF16)
        wo = A.alloc((8, 1024), BF16)
        gt1bc = A.alloc((1024,), F32)
        diag = A.alloc((128,), F32)
        oin = [A.alloc((8, 512), BF16) for _ in range(2)]
        bin_ = [A.alloc((4, 512), BF16) for _ in range(2)]
        gin = [A.alloc((16, 512), BF16) for _ in range(2)]
        mT = A.alloc((8, 512), BF16)
        ct1 = [A.alloc((512,), F32) for _ in range(2)]
        ct2 = [A.alloc((512,), F32) for _ in range(2)]
        xin = [A.alloc((1024,), F32) for _ in range(4)]
        x1o = [A.alloc((1024,), F32) for _ in range(2)]
        P.dma("pool", wa, wa_d.rearrange("(k p) n -> p k n", p=128), "wa", writes=["wa"])
        P.dma("pool", wb, wb_d.rearrange("(k p) n -> p k n", p=128), "wb", writes=["wb"])
        P.dma("pool", wo, wo_d.rearrange("(k p) n -> p k n", p=128), "wo", writes=["wo"])
        prep_gt(s, 0, gt1bc, diag)
        oaT_v = oaT_scr.rearrange("h p s -> p h s")
        obT_v = obT_scr.rearrange("h p s -> p h s")
        sg_v = sg_scr.rearrange("h p s -> p h s")
        xc = [0]
        def c_loads(t):
            sl_ = t % 2
            P.dma("sp", oin[sl_], oaT_v[:, :, t * 512:(t + 1) * 512], f"oin{sl_}", writes=[("oin", sl_)])
            P.dma("sp", bin_[sl_], obT_v[:, :, t * 512:(t + 1) * 512], f"bin{sl_}", writes=[("bin", sl_)])
            P.dma("sp", gin[sl_], sg_v[:, :, t * 512:(t + 1) * 512], f"gin{sl_}", writes=[("gin", sl_)])

        c_loads(0)
        for t in range(nt):
            sl = t % 2
            if t + 1 < nt:
                c_loads(t + 1)
            for bb in range(4):
                r0_ = t0 + t * 512 + bb * 128
                P.dma("sp", xin[bb], x_d[r0_:r0_ + 128, :], f"xin{bb}", writes=[("xin", bb)])
            for m in range(8):
                ba = m % 2
                bb_ = 2 + m % 2
                for hh in range(8):
                    mm(pbank[ba][:], wa[:, hh, m * 128:(m + 1) * 128], oin[sl][:, hh, :], hh == 0, hh == 7, ["wa", ("oin", sl)], bk(ba))
                for pp in range(4):
                    mm(pbank[bb_][:], wb[:, pp, m * 128:(m + 1) * 128], bin_[sl][:, pp, :], pp == 0, pp == 3, ["wb", ("bin", sl)], bk(bb_))
                tt("dve", ct1[m % 2], pbank[ba][:], gin[sl][:, m, :], ALU.mult, bk(ba) + [("gin", sl)], [("ct1", m % 2)])
                tt("dve", ct2[m % 2], pbank[bb_][:], gin[sl][:, 8 + m, :], ALU.mult, bk(bb_) + [("gin", sl)], [("ct2", m % 2)])
                tt("pool", mT[:, m, :], ct1[m % 2], ct2[m % 2], ALU.add, [("ct1", m % 2), ("ct2", m % 2)], [("mT", m)])
            for bb in range(4):
                xs = bb
                xo = xc[0] % 2
                xc[0] += 1
                r0 = t0 + t * 512 + bb * 128
                for half in range(2):
                    py = 4 + (2 * bb + half) % 4
                    for m in range(8):
                        mm(pbank[py][:], mT[:, m, bb * 128:(bb + 1) * 128], wo[:, m, half * 512:(half + 1) * 512], m == 0, m == 7,
                           [("mT", m), "wo"], bk(py))
                    tt("dve", ct1[half], pbank[py][:], gt1bc[:, half * 512:(half + 1) * 512], ALU.mult, bk(py) + ["gtbc"], [("ct1", half)])
                    tt("pool", x1o[xo][:, half * 512:(half + 1) * 512], ct1[half], xin[xs][:, half * 512:(half + 1) * 512], ALU.add,
                       [("ct1", half), ("xin", xs)], [("x1o", xo)])
                P.dma("sp", x1_scr[r0:r0 + 128, :], x1o[xo], f"x1o{xo}", reads=[("x1o", xo)], writes=[("x1_scr", r0)])
        P.barrier()

    if stages >= 6:
        A.off = persist_mark
        wgu = A.alloc((8, 2 * FF), BF16)
        wdn = A.alloc((22, 1024), BF16)
        gt2bc = A.alloc((1024,), F32)
        gfin = A.alloc((1024,), F32)
        h2T = A.alloc((8, 512), BF16)
        actT = A.alloc((22, 512), BF16)
        x1in = [A.alloc((1024,), F32) for _ in range(4)]
        hn2 = [A.alloc((1024,), BF16)] * 2
        junk = hn2[0]
        tmpA = A.alloc((8, 128), F32)
        diag = tmpA[:, 0, :]
        et = [A.alloc((512,), F32) for _ in range(2)]
        x2 = [A.alloc((1024,), F32)] * 2
        wgu_v = wgu_d.rearrange("(k p) n -> p k n", p=128)
        wdn_v = wdn_d.rearrange("(k p) n -> p k n", p=128)
        for c in range(11):
            P.dma("pool", wgu[:, :, c * 512:(c + 1) * 512], wgu_v[:, :, c * 512:(c + 1) * 512], f"wgu{c}", writes=[("wgu", c)])
        P.dma("pool", wdn[:, 0:11, :], wdn_v[:, 0:11, :], "wdn0", writes=[("wdn", 0)])
        P.dma("pool", wdn[:, 11:22, :], wdn_v[:, 11:22, :], "wdn1", writes=[("wdn", 1)])
        ld(gfin, gfin_d.partition_broadcast(128), "gfin", "gfin")
        wgu_all = [("wgu", c) for c in range(11)]
        x1n = [x1in[0], x1in[1]]
        xr = [x1in[2], x1in[3], x2[0]]
        tiles = [(s_, t_) for s_ in range(nseq) for t_ in range(seq_lens[s_] // 512)]
        nctr = [0]
        rctr = [0]
        pend_nb = {}

        nslot = {}

        def normA_load(idx, bb):
            s_, t_ = tiles[idx]
            xs = nctr[0] % 2
            nctr[0] += 1
            nslot[(idx, bb)] = xs
            r0_ = seq_off[s_] + t_ * 512 + bb * 128
            P.dma("sp", x1n[xs], x1_scr[r0_:r0_ + 128, :], f"x1n{xs}", writes=[("x1n", xs)])

        def normA_blk(idx, bb):
            s_, t_ = tiles[idx]
            if t_ == 0 and bb == 0:
                prep_m(s_, 1)
            if (idx, bb) not in nslot:
                normA_load(idx, bb)
            xs = nslot.pop((idx, bb))
            norm_A(x1n[xs], ("x1n", xs), hn2[0], ("hn2", 0), 4 * (bb % 2), junk, jname=("hn2", 0))

        def normB_blk(idx, bb):
            s_, t_ = tiles[idx]
            norm_B(hn2[0], ("hn2", 0), tmpA, h2T, [("h2T", bb)], 1, s_, bb * 128, bb % 2)

        for bb in range(4):
            normA_blk(0, bb)
            normB_blk(0, bb)
        for idx, (s, t) in enumerate(tiles):
            t0 = seq_off[s]
            if t == 0:
                prep_gt(s, 1, gt2bc, diag, "tmpA")
            h2names = [("h2T", bb) for bb in range(4)]
            for f in range(22):
                bg = 2 + f % 2
                bu = 4 + f % 2
                for k in range(8):
                    mm(pbank[bg][:], wgu[:, k, f * 128:(f + 1) * 128], h2T[:, k, :], k == 0, k == 7, wgu_all + h2names, bk(bg))
                for k in range(8):
                    mm(pbank[bu][:], wgu[:, k, FF + f * 128:FF + (f + 1) * 128], h2T[:, k, :], k == 0, k == 7, wgu_all + h2names, bk(bu))
                e_ = et[f % 2]
                en = ("et", f % 2)
                act(e_, pbank[bg][:], AF.Silu, bk(bg), [en])
                tt("dve", actT[:, f, :], e_, pbank[bu][:], ALU.mult, [en] + bk(bu), [("actT", f)])
            anames = [("actT", f) for f in range(22)]
            nxt = idx + 1 < len(tiles)
            rslot = {}

            def issue_loads(bb_):
                xo_ = rctr[0] % 3
                rctr[0] += 1
                rslot[bb_] = xo_
                r0_ = t0 + t * 512 + bb_ * 128
                P.dma("sp", xr[xo_], x1_scr[r0_:r0_ + 128, :], f"xr{xo_}", writes=[("xr", xo_)])
                if nxt:
                    normA_load(idx + 1, bb_)

            issue_loads(0)
            for bb in range(4):
                if bb + 1 < 4:
                    issue_loads(bb + 1)
                xo = rslot[bb]
                r0 = t0 + t * 512 + bb * 128
                if nxt:
                    normA_blk(idx + 1, bb)
                for half in range(2):
                    py = 6 + half
                    for f in range(22):
                        mm(pbank[py][:], actT[:, f, bb * 128:(bb + 1) * 128], wdn[:, f, half * 512:(half + 1) * 512], f == 0, f == 21,
                           anames + [("wdn", 0), ("wdn", 1)], bk(py))
                    tt("dve", et[half], pbank[py][:], gt2bc[:, half * 512:(half + 1) * 512], ALU.mult, bk(py) + ["gtbc"], [("et", half)])
                    tt("pool", xr[xo][:, half * 512:(half + 1) * 512], et[half], xr[xo][:, half * 512:(half + 1) * 512], ALU.add,
                       [("et", half), ("xr", xo)], [("xr", xo)])
                if nxt:
                    normB_blk(idx + 1, bb)
                so = 16 + 4 * xo
                ss = small[:, so:so + 1]
                P.op("act", lambda e, a=ss: e.memzero(a), (), [("small", so)])
                act(junk, xr[xo], AF.Square, [("xr", xo), ("small", so)], [("hn2", 0), ("small", so)], accum=ss)
                act(small[:, so + 1:so + 2], ss, AF.Ln, [("small", so)], [("small", so + 1)], bias=epst, scale=1.0 / D)
                act(small[:, so + 2:so + 3], small[:, so + 1:so + 2], AF.Exp, [("small", so + 1)], [("small", so + 2)], scale=-0.5)
                stt("dve", xr[xo], xr[xo], small[:, so + 2:so + 3], gfin, ALU.mult, ALU.mult,
                    [("xr", xo), ("small", so + 2), "gfin"], [("xr", xo)])
                P.dma("sp", y_d[r0:r0 + 128, :], xr[xo], f"yo{xo}", reads=[("xr", xo)])
        final_keys = ["yo0", "yo1", "yo2"]
    else:
        final_keys = []
    print("ops:", {e: len(P.streams[e]) for e in ENGS}, "arena", A.off)
    with nc.allow_low_precision(reason="bf16 matmul operands by design"):
        P.emit(final_wait_keys=final_keys)
    return nc


def make_consts():
    bf = ml_dtypes.bfloat16
    ident = np.eye(128, dtype=np.float32)
    rmatT = np.zeros((128, 128), np.float32)
    for b in (0, 64):
        for i in range(8):
            rmatT[b + i + 8, b + i] = -1.0
            rmatT[b + i, b + i + 8] = 1.0
    kk = np.arange(128)[:, None]
    qq = np.arange(128)[None, :]
    mA = (kk >= qq).astype(np.float32)
    mB = (kk <= qq).astype(np.float32)
    mA0 = ((kk + 64 >= qq) & (kk < 64)).astype(np.float32)
    m_reg = np.stack([mA, mB, mA, mB], axis=1)
    m_first = np.stack([mA0, mB, mA0, mB], axis=1)
    mask4 = (np.stack([m_reg, m_first], axis=1).reshape(128, 1024) - 1.0) * 30000.0
    inv = (np.float32(1.0) / (np.float32(500000.0) ** (np.arange(0, 16, 2, dtype=np.float32) / np.float32(16)))).astype(np.float32)
    ang = (np.arange(4096, dtype=np.float32)[:, None] * inv[None, :]).astype(np.float32)
    cos = np.cos(ang).astype(np.float32)
    sin = np.sin(ang).astype(np.float32)
    ropeC = np.ones((128, 4096), np.float32)
    ropeS = np.zeros((128, 4096), np.float32)
    for b in (0, 64):
        for i in range(8):
            ropeC[b + i] = cos[:, i]
            ropeC[b + i + 8] = cos[:, i]
            ropeS[b + i] = sin[:, i]
            ropeS[b + i + 8] = sin[:, i]
    return {
        "ident": ident.astype(bf), "identf": ident, "rmatT": rmatT.astype(bf), "mask4": mask4.astype(bf),
        "ropeC": ropeC, "ropeS": ropeS,
    }


def make_in_map(xs, cs, w, consts):
    m = dict(consts)
    m["x"] = np.ascontiguousarray(np.concatenate(xs, axis=0), dtype=np.float32)
    cT = np.stack(cs, axis=0).reshape(len(cs), 8, 128).transpose(2, 1, 0)
    m["cT"] = np.ascontiguousarray(cT.reshape(128, 8 * len(cs)), dtype=np.float32)
    m["w_ada"] = w["w_ada"][0]
    m["b_adaF"] = np.ascontiguousarray(w["b_ada"][0].reshape(48, 128).T)
    m["g_mixF"] = np.ascontiguousarray(w["g_mix"][0].reshape(8, 128).T)
    m["g_ffnF"] = np.ascontiguousarray(w["g_ffn"][0].reshape(8, 128).T)
    m["w_in"] = w["w_in"][0]
    for n in ("lam_q1", "lam_k1", "lam_q2", "lam_k2", "g_subln", "w_a", "w_b", "w_out", "w_gu", "w_down"):
        m[n] = w[n][0]
    m["g_final"] = w["g_final"]
    return {k: np.ascontiguousarray(v) for k, v in m.items()}


def kernel(**inputs):
    inp = {k: np.asarray(v) for k, v in inputs.items()}
    xp, xs_ = inp["x_prompt"], inp["x_sample"]
    cp_, cs_ = inp["c_prompt"], inp["c_sample"]
    n = 8
    seq_lens = [xp.shape[1], xp.shape[1], xs_.shape[1], xs_.shape[1]]
    nc = build_program(seq_lens)
    consts = make_consts()
    in_maps = []
    for c in range(n):
        xs = [xp[2 * c], xp[2 * c + 1], xs_[2 * c], xs_[2 * c + 1]]
        cs = [cp_[2 * c], cp_[2 * c + 1], cs_[2 * c], cs_[2 * c + 1]]
        in_maps.append(make_in_map(xs, cs, inp, consts))
    res = run_bass_kernel_spmd(nc, in_maps, core_ids=list(range(n)))
    yp = np.empty_like(xp, dtype=np.float32)
    ys = np.empty_like(xs_, dtype=np.float32)
    Sp, Ss = xp.shape[1], xs_.shape[1]
    for c in range(n):
        y = res.results[c]["y"]
        yp[2 * c] = y[0:Sp]
        yp[2 * c + 1] = y[Sp:2 * Sp]
        ys[2 * c] = y[2 * Sp:2 * Sp + Ss]
        ys[2 * c + 1] = y[2 * Sp + Ss:2 * Sp + 2 * Ss]
    return (yp, ys)
```

```python
import contextlib
import numpy as np
import ml_dtypes
import concourse.bass as bass
import concourse.mybir as mybir
from concourse.bass_utils import run_bass_kernel_spmd

F32 = mybir.dt.float32
BF16 = mybir.dt.bfloat16
U8 = mybir.dt.uint8
ALU = mybir.AluOpType
AF = mybir.ActivationFunctionType
AX = mybir.AxisListType

D = 1024
NIN = 9728
FF = 2816
EPS = 1e-6
ENGS = ("pe", "act", "dve", "pool", "sp")
SEM_WRAP = 20000
import os
DBG_SKIP = os.environ.get('DBG_SKIP', '')
DBG_GROUPS = [int(c) for c in os.environ.get('DBG_GROUPS', '012')]


class Op:
    __slots__ = ("eng", "fn", "deps", "is_dma", "key", "dval", "sig", "signals")

    def __init__(self, eng, fn, is_dma=False, key=None):
        self.eng = eng
        self.fn = fn
        self.deps = []
        self.is_dma = is_dma
        self.key = key
        self.dval = 0
        self.sig = 0
        self.signals = False


class Prog:
    def __init__(self, nc):
        self.nc = nc
        self.streams = {e: [] for e in ENGS}
        self.wr = {}
        self.rd = {}
        self.dma_cnt = {}
        self.last_dma = {}
        self.bar = {}
        self.same_eng_sync = {"act", "dve", "pool"}

    def _src(self, op):
        return ("dma", id(op)) if op.is_dma else op.eng

    def _add(self, op, d, raw):
        if d is op:
            return
        if (not d.is_dma) and (not op.is_dma) and d.eng == op.eng:
            if not (raw and d.eng in self.same_eng_sync):
                return
        op.deps.append(d)
        if not d.is_dma:
            d.signals = True

    def _track(self, op, reads, writes):
        pbr = [x for x in reads if isinstance(x, tuple) and x and x[0] == "pb"]
        if pbr:
            reads = [x for x in reads if x not in pbr]
            writes = list(writes) + [x for x in pbr if x not in writes]
        b = self.bar.pop(op.eng, None)
        if b:
            for d in b:
                self._add(op, d, True)
        for x in reads:
            for o in self.wr.get(x, {}).values():
                self._add(op, o, True)
        for x in writes:
            for o in self.wr.get(x, {}).values():
                self._add(op, o, False)
            for o in self.rd.get(x, {}).values():
                self._add(op, o, False)
        src = self._src(op)
        for x in writes:
            self.wr[x] = {src: op}
            self.rd[x] = {}
        for x in reads:
            if x in writes:
                continue
            self.rd.setdefault(x, {})[src] = op

    def op(self, eng, fn, reads=(), writes=()):
        o = Op(eng, fn)
        self._track(o, reads, writes)
        self.streams[eng].append(o)
        return o

    def dma(self, queue, out_ap, in_ap, key, reads=(), writes=()):
        def fn(e, out_ap=out_ap, in_ap=in_ap):
            return e.dma_start(out=out_ap, in_=in_ap)
        o = Op(queue, fn, is_dma=True, key=key)
        self.dma_cnt[key] = self.dma_cnt.get(key, 0) + 1
        o.dval = 16 * self.dma_cnt[key]
        self._track(o, reads, writes)
        self.streams[queue].append(o)
        self.last_dma[key] = o
        return o

    def barrier(self):
        lasts = []
        for e in ENGS:
            for o in reversed(self.streams[e]):
                if not o.is_dma:
                    o.signals = True
                    lasts.append(o)
                    break
        lasts += list(self.last_dma.values())
        self.bar = {e: list(lasts) for e in ENGS}
        self.wr = {}
        self.rd = {}

    def emit(self, final_wait_keys=()):
        nc = self.nc
        nsig = {}
        for e in ENGS:
            c = 0
            for o in self.streams[e]:
                if o.signals and not o.is_dma:
                    c += 1
                    o.sig = c
            nsig[e] = c
        with contextlib.ExitStack() as es:
            esems = {}
            for e in ENGS:
                n = max((nsig[e] + SEM_WRAP - 1) // SEM_WRAP, 1)
                esems[e] = [es.enter_context(nc.semaphore(f"s_{e}_{j}")) for j in range(n)]
            dsems = {}
            for k in self.dma_cnt:
                dsems[k] = es.enter_context(nc.semaphore(f"d_{len(dsems)}"))
            block = es.enter_context(nc.Block())
            streams = self.streams
            dma_cnt = self.dma_cnt

            def run(ename, eng):
                waited = {}
                for o in streams[ename]:
                    need = {}
                    for d in o.deps:
                        if d.is_dma:
                            k = ("d", d.key)
                            v = d.dval
                        else:
                            k = ("e", d.eng)
                            v = d.sig
                        if v > need.get(k, 0):
                            need[k] = v
                    for k, v in need.items():
                        if waited.get(k, 0) >= v:
                            continue
                        waited[k] = v
                        if k[0] == "d":
                            eng.wait_ge(dsems[k[1]], v)
                        else:
                            j = (v - 1) // SEM_WRAP
                            eng.wait_ge(esems[k[1]][j], (v - 1) % SEM_WRAP + 1)
                    ins = o.fn(eng)
                    if o.is_dma:
                        ins.then_inc(dsems[o.key], 16)
                    elif o.signals:
                        j = (o.sig - 1) // SEM_WRAP
                        ins.then_inc(esems[ename][j], 1)
                if ename == "sp":
                    for k in final_wait_keys:
                        eng.wait_ge(dsems[k], 16 * dma_cnt[k])

            @block.tensor
            def _(e):
                run("pe", e)

            @block.scalar
            def _(e):
                run("act", e)

            @block.vector
            def _(e):
                run("dve", e)

            @block.gpsimd
            def _(e):
                run("pool", e)

            @block.sync
            def _(e):
                run("sp", e)


class Arena:
    def __init__(self, nc, nbytes):
        self.t = nc.alloc_sbuf_tensor("arena", [128, nbytes], U8)
        self.n = nbytes
        self.off = 0

    def alloc(self, free_shape, dtype):
        esz = 2 if dtype == BF16 else 4
        n = int(np.prod(free_shape)) * esz
        n_al = (n + 63) // 64 * 64
        assert self.off + n_al <= self.n, f"arena overflow {self.off + n_al} > {self.n}"
        v = self.t[:, self.off:self.off + n].bitcast(dtype)
        self.off += n_al
        if len(free_shape) == 2:
            v = v.rearrange("p (a b) -> p a b", b=free_shape[1])
        elif len(free_shape) == 3:
            v = v.rearrange("p (a b c) -> p a b c", b=free_shape[1], c=free_shape[2])
        return v


def build_program(seq_lens, stages=99):
    nseq = len(seq_lens)
    T = sum(seq_lens)
    SMAX = max(seq_lens)
    nc = bass.Bass("TRN2", target_bir_lowering=False)

    def din(name, shape, dt=F32):
        return nc.dram_tensor(name, list(shape), dt, kind="ExternalInput").ap()

    x_d = din("x", [T, D])
    cT_d = din("cT", [128, 8 * nseq])
    wada_d = din("w_ada", [D, 6 * D])
    badaF_d = din("b_adaF", [128, 48])
    gmixF_d = din("g_mixF", [128, 8])
    gffnF_d = din("g_ffnF", [128, 8])
    win_d = din("w_in", [D, NIN])
    lam_d = [din(n, [64]) for n in ("lam_q1", "lam_k1", "lam_q2", "lam_k2")]
    gsub_d = din("g_subln", [128])
    wa_d = din("w_a", [D, D])
    wb_d = din("w_b", [512, D])
    wo_d = din("w_out", [D, D])
    wgu_d = din("w_gu", [D, 2 * FF])
    wdn_d = din("w_down", [FF, D])
    gfin_d = din("g_final", [D])
    ident_d = din("ident", [128, 128], BF16)
    identf_d = din("identf", [128, 128])
    rmat_d = din("rmatT", [128, 128], BF16)
    mask_d = din("mask4", [128, 1024], BF16)
    ropeC_d = din("ropeC", [128, 4096])
    ropeS_d = din("ropeS", [128, 4096])
    y_d = nc.dram_tensor("y", [T, D], F32, kind="ExternalOutput").ap()
    oaT_scr = nc.dram_tensor("oaT_scr", [8, 128, SMAX], BF16, kind="Internal").ap()
    obT_scr = nc.dram_tensor("obT_scr", [4, 128, SMAX], BF16, kind="Internal").ap()
    sg_scr = nc.dram_tensor("sg_scr", [16, 128, SMAX], BF16, kind="Internal").ap()
    x1_scr = nc.dram_tensor("x1_scr", [T, D], F32, kind="Internal").ap()

    P = Prog(nc)
    A = Arena(nc, 212480)
    pbank = [nc.alloc_psum_tensor(f"pb{i}", [128, 512], F32) for i in range(8)]

    def bk(i, qs=None):
        return [("pb", i)]

    def mm(out, lhsT, rhs, start, stop, reads, writes):
        P.op("pe", lambda e, o=out, l=lhsT, r=rhs, s=start, t=stop: e.matmul(o, l, r, start=s, stop=t), reads, writes)

    def tr(out, in_, idn, reads, writes):
        P.op("pe", lambda e, o=out, i=in_, d=idn: e.transpose(o, i, d), reads, writes)

    def act(out, in_, func, reads, writes, bias=None, scale=None, accum=None):
        kw = {}
        if bias is not None:
            kw["bias"] = bias
        if scale is not None:
            kw["scale"] = scale
        if accum is not None:
            kw["accum_out"] = accum
        P.op("act", lambda e, o=out, i=in_, f=func, kw=kw: e.activation(o, i, f, **kw), reads, writes)

    def tt(eng, out, in0, in1, op, reads, writes):
        P.op(eng, lambda e, o=out, a=in0, b=in1, p=op: e.tensor_tensor(o, a, b, p), reads, writes)

    def ts(eng, out, in0, s1, s2, op0, op1, reads, writes):
        if s2 is None:
            P.op(eng, lambda e, o=out, a=in0, s=s1, p=op0: e.tensor_scalar(o, a, s, None, p), reads, writes)
        else:
            P.op(eng, lambda e, o=out, a=in0, s=s1, s_=s2, p=op0, q=op1: e.tensor_scalar(o, a, s, s_, p, q), reads, writes)

    def stt(eng, out, in0, sc, in1, op0, op1, reads, writes):
        P.op(eng, lambda e, o=out, a=in0, s=sc, b=in1, p=op0, q=op1: e.scalar_tensor_tensor(o, a, s, b, p, q), reads, writes)

    def cp(eng, out, in_, reads, writes):
        P.op(eng, lambda e, o=out, i=in_: e.tensor_copy(o, i), reads, writes)

    def recip(out, in_, reads, writes):
        P.op("dve", lambda e, o=out, i=in_: e.reciprocal(o, i), reads, writes)

    def mset(eng, ap, val, writes):
        P.op(eng, lambda e, a=ap, v=val: e.memset(a, v), (), writes)

    def rsum(out, in_, reads, writes):
        P.op("dve", lambda e, o=out, i=in_: e.reduce_sum(o, i, AX.X), reads, writes)

    ident = A.alloc((128,), BF16)
    identf = A.alloc((128,), F32)
    rmatT = A.alloc((128,), BF16)
    mask4 = A.alloc((2, 4, 128), BF16)
    onesf = A.alloc((128,), F32)
    epst = A.alloc((1,), F32)
    cT = A.alloc((8, nseq), F32)
    siluT = A.alloc((8, nseq), F32)
    badaF = A.alloc((48,), F32)
    gmixF = A.alloc((8,), F32)
    gffnF = A.alloc((8,), F32)
    modF = A.alloc((48, nseq), F32)
    lamt = A.alloc((4, 64), F32)
    lams = A.alloc((8,), F32)
    neglam = A.alloc((1,), F32)
    gsubbc = A.alloc((128,), F32)
    ones_b = A.alloc((128,), BF16)
    gsubF = A.alloc((1,), F32)
    mvec = A.alloc((4, 8), F32)
    small = A.alloc((64,), F32)
    persist_mark = A.off

    def ld(dst, src, key, wname):
        P.dma("sp", dst, src, key, writes=[wname])

    ld(ident, ident_d, "c0", "ident")
    ld(identf, identf_d, "c1", "identf")
    ld(rmatT, rmat_d, "c2", "rmatT")
    ld(mask4.rearrange("p a b c -> p (a b c)"), mask_d, "c3", "mask4")
    ld(cT.rearrange("p a b -> p (a b)"), cT_d, "c4", "cT")
    ld(badaF, badaF_d, "c5", "badaF")
    ld(gmixF, gmixF_d, "c6", "gmixF")
    ld(gffnF, gffnF_d, "c7", "gffnF")
    for i in range(4):
        ld(lamt[:, i, :], lam_d[i].partition_broadcast(128), f"c8_{i}", ("lamt", i))
    ld(gsubbc, gsub_d.partition_broadcast(128), "c9", "gsubbc")
    mset("pool", onesf, 1.0, ["onesf"])
    mset("pool", ones_b, 1.0, ["ones_b"])
    ld(gsubF, gsub_d.rearrange("(p o) -> p o", o=1), "c10", "gsubF")
    ts("dve", gsubF, gsubF, 0.8, None, ALU.mult, None, ["gsubF"], ["gsubF"])
    mset("pool", epst, EPS, ["epst"])

    for j in range(2):
        tt("dve", lamt[:, 2 * j, :], lamt[:, 2 * j, :], lamt[:, 2 * j + 1, :], ALU.mult,
           [("lamt", 2 * j), ("lamt", 2 * j + 1)], [("lamt", 2 * j)])
        rsum(lams[:, j:j + 1], lamt[:, 2 * j, :], [("lamt", 2 * j)], [("lams", j)])
        act(lams[:, 2 + j:3 + j], lams[:, j:j + 1], AF.Exp, [("lams", j)], [("lams", 2 + j)])
    tt("dve", lams[:, 4:5], lams[:, 3:4], lams[:, 2:3], ALU.subtract, [("lams", 2), ("lams", 3)], [("lams", 4)])
    ts("dve", neglam, lams[:, 4:5], -0.2, None, ALU.add, None, [("lams", 4)], ["neglam"])
    ts("dve", gsubbc, gsubbc, 0.8, None, ALU.mult, None, ["gsubbc"], ["gsubbc"])
    cTf = cT.rearrange("p a b -> p (a b)")
    sTf = siluT.rearrange("p a b -> p (a b)")
    act(sTf, cTf, AF.Exp, ["cT"], ["siluT"], scale=-1.0)
    ts("dve", sTf, sTf, 1.0, None, ALU.add, None, ["siluT"], ["siluT"])
    recip(sTf, sTf, ["siluT"], ["siluT"])
    tt("dve", sTf, sTf, cTf, ALU.mult, ["siluT", "cT"], ["siluT"])

    A.off = persist_mark
    wada = [A.alloc((8, 512), F32) for _ in range(2)]
    wada_v = wada_d.rearrange("(k p) n -> p k n", p=128)
    for i in range(12):
        sl = i % 2
        P.dma("sp", wada[sl], wada_v[:, :, i * 512:(i + 1) * 512], f"wada{sl}", writes=[("wada", sl)])
        for sub in range(4):
            j = i * 4 + sub
            b = j % 2
            for k in range(8):
                mm(pbank[b][:, 0:nseq], wada[sl][:, k, sub * 128:(sub + 1) * 128], siluT[:, k, :],
                   k == 0, k == 7, [("wada", sl), "siluT"], bk(b))
            ts("dve", modF[:, j, :], pbank[b][:, 0:nseq], badaF[:, j:j + 1], None, ALU.add, None,
               bk(b) + ["badaF"], [("modF", j)])
    P.barrier()

    def bc3(v, n):
        return v.rearrange("p (k o) -> p k o", o=1).to_broadcast([128, 8, n])

    def prep_m(s, which):
        base = 8 if which == 0 else 32
        g = gmixF if which == 0 else gffnF
        stt("dve", mvec[:, 2 * which, :], modF[:, base:base + 8, s], 1.0, g, ALU.add, ALU.mult,
            [("modF", j) for j in range(base, base + 8)] + ["gmixF", "gffnF"], [("mvec", which)])

    def prep_gt(s, which, gtbc, diag, dname="diag"):
        base = 16 if which == 0 else 40
        for k in range(8):
            ts("dve", diag, identf, modF[:, base + k, s:s + 1], None, ALU.mult, None,
               ["identf", ("modF", base + k)], [dname])
            mm(pbank[k % 2][:, 0:128], onesf, diag, True, True, ["onesf", dname], bk(k % 2))
            cp("dve", gtbc[:, k * 128:(k + 1) * 128], pbank[k % 2][:, 0:128], bk(k % 2), ["gtbc"])

    def norm_A(xin, xname, hn, hnname, ssl, junk, jname="junk"):
        ss = small[:, ssl:ssl + 1]
        lnv = small[:, ssl + 1:ssl + 2]
        rstd = small[:, ssl + 2:ssl + 3]
        P.op("act", lambda e, a=ss: e.memzero(a), (), [("small", ssl)])
        act(junk, xin, AF.Square, [xname, ("small", ssl)], [jname, ("small", ssl)], accum=ss)
        act(lnv, ss, AF.Ln, [("small", ssl)], [("small", ssl + 1)], bias=epst, scale=1.0 / D)
        act(rstd, lnv, AF.Exp, [("small", ssl + 1)], [("small", ssl + 2)], scale=-0.5)
        act(hn, xin, AF.Copy, [xname, ("small", ssl + 2)], [hnname], scale=rstd)

    def norm_B(hn, hnname, tmpA, dstT, dst_names, which, s, blkcol, pb, tname="tmpA"):
        pv = pbank[pb][:].bitcast(BF16)
        for k in range(8):
            tr(pv[:, k * 128:(k + 1) * 128], hn[:, k * 128:(k + 1) * 128], ident, [hnname, "ident"], bk(pb))
        shb = 0 if which == 0 else 24
        tt("dve", tmpA, pv.rearrange("p (k n) -> p k n", n=128), bc3(mvec[:, 2 * which, :], 128), ALU.mult,
           bk(pb) + [("mvec", which)], [tname])
        tt("dve", dstT[:, :, blkcol:blkcol + 128], tmpA, bc3(modF[:, shb:shb + 8, s], 128), ALU.add,
           [tname] + [("modF", j) for j in range(shb, shb + 8)], dst_names)

    def norm_to_T(xin, xname, hn, hnname, ssl, junk, tmpA, dstT, dst_names, which, s, blkcol, pb, jname="junk", tname="tmpA"):
        norm_A(xin, xname, hn, hnname, ssl, junk, jname)
        norm_B(hn, hnname, tmpA, dstT, dst_names, which, s, blkcol, pb, tname)

    seq_off = [int(sum(seq_lens[:i])) for i in range(nseq)]

    for s in range(nseq if stages >= 1 else 0):
        S = seq_lens[s]
        nb = S // 128
        nt = S // 512
        t0 = seq_off[s]
        A.off = persist_mark
        ropeC = A.alloc((4096,), F32)
        ropeS = A.alloc((4096,), F32)
        hT = A.alloc((8, SMAX), BF16)
        ab_mark = A.off
        ld(ropeC, ropeC_d, "rc", "ropeC")
        ld(ropeS, ropeS_d, "rs", "ropeS")
        prep_m(s, 0)

        xt = [A.alloc((1024,), F32) for _ in range(3)]
        hn = [A.alloc((1024,), BF16) for _ in range(2)]
        junk = A.alloc((1024,), BF16)
        tmpA2 = [A.alloc((8, 128), F32) for _ in range(2)]
        for blk in range(nb):
            sl = blk % 3
            P.dma("sp", xt[sl], x_d[t0 + blk * 128:t0 + (blk + 1) * 128, :], f"xt{sl}", writes=[("xt", sl)])
            norm_to_T(xt[sl], ("xt", sl), hn[blk % 2], ("hn", blk % 2), 4 * (blk % 2), junk, tmpA2[blk % 2], hT,
                      [("hT", blk // 4)], 0, s, blk * 128, blk % 2, tname=("tmpA", blk % 2))

        P.barrier()
        if stages < 2:
            continue
        A.off = ab_mark
        wsl = [[A.alloc((8, 128), BF16) for _ in range(3)] for _ in range(2)]
        KT = A.alloc((SMAX,), BF16)
        QT = A.alloc((SMAX,), BF16)
        KT1 = A.alloc((SMAX,), BF16)
        KTz = [KT, KT1]
        mset("pool", KT[64:128, :], 0.0, [("KTpad", 0)])
        mset("pool", KT1[0:64, :], 0.0, [("KTpad", 1)])
        zb = [A.alloc((512,), BF16) for _ in range(2)]
        rt_off = A.off
        rtb = A.alloc((2048,), F32)
        rt1 = [rtb[:, i * 512:(i + 1) * 512] for i in range(2)]
        rt2 = [rtb[:, (2 + i) * 512:(3 + i) * 512] for i in range(2)]
        VTv = A.t[:, rt_off:rt_off + 8192].bitcast(BF16)
        rt_names = [("rt1", 0), ("rt1", 1), ("rt2", 0), ("rt2", 1)]
        PT = [A.alloc((512,), BF16) for _ in range(4)]
        stage = [A.alloc((512,), BF16) for _ in range(2)]
        Vreg_off = A.off
        Vext = A.alloc((32, 129), BF16)
        A.off = Vreg_off
        Vd = A.alloc((33, 2, 128), BF16)
        b_mark = A.off
        win_v = win_d.rearrange("(k p) n -> p k n", p=128)
        ucount = [0]

        def load_w(cols):
            u = ucount[0] % 2
            ucount[0] += 1
            for j, c0 in enumerate(cols):
                P.dma("pool", wsl[u][j], win_v[:, :, c0:c0 + 128], f"w{u}{j}", writes=[("w", u, j)])
            return u

        def hT_names(tok_lo, tok_hi):
            return [("hT", t) for t in range(tok_lo // 512, (tok_hi - 1) // 512 + 1)]

        pz = [0]

        def proj_T_rope(u, j, dst, dname, split=None, perm=1):
            bs = {}

            def stA(t):
                b = pz[0] % 2
                pz[0] += 1
                bs[t] = b
                for k in range(8):
                    mm(pbank[b][:], wsl[u][j][:, k, :], hT[:, k, t * 512:(t + 1) * 512], k == 0, k == 7,
                       [("w", u, j), ("hT", t)], bk(b))
                act(zb[b], pbank[b][:], AF.Copy, bk(b), [("zb", b)])

            def stB(t):
                b = bs.pop(t)
                rb = 6 + b
                mm(pbank[rb][:], rmatT, zb[b], True, True, ["rmatT", ("zb", b)], bk(rb))
                tt("dve", rt1[b], pbank[b][:], ropeC[:, t * 512:(t + 1) * 512], ALU.mult, bk(b) + ["ropeC"], [("rt1", b)])
                tt("dve", rt2[b], pbank[rb][:], ropeS[:, t * 512:(t + 1) * 512], ALU.mult, bk(rb) + ["ropeS"], [("rt2", b)])
                def dsl(d_ap, p0, p1):
                    if perm == 1:
                        return d_ap[p0:p1, t * 512:(t + 1) * 512]
                    w_ = 512 // perm
                    return d_ap[p0:p1, 0:S].rearrange("p (r t) -> p r t", r=perm)[:, :, t * w_:(t + 1) * w_]

                def ssl(s_ap, p0, p1):
                    if perm == 1:
                        return s_ap[p0:p1, :]
                    return s_ap[p0:p1, :].rearrange("p (a r) -> p r a", r=perm)

                if split is None:
                    tt("pool", dsl(dst, 0, 128), ssl(rt1[b], 0, 128), ssl(rt2[b], 0, 128), ALU.add, [("rt1", b), ("rt2", b)], [(dname, t)])
                else:
                    for hd_ in range(2):
                        tt("pool" if hd_ == 0 else "dve", dsl(split[hd_], hd_ * 64, (hd_ + 1) * 64),
                           ssl(rt1[b], hd_ * 64, (hd_ + 1) * 64), ssl(rt2[b], hd_ * 64, (hd_ + 1) * 64), ALU.add,
                           [("rt1", b), ("rt2", b)], [(dname, t, hd_)])

            for t in range(nt + 1):
                if t < nt:
                    stA(t)
                if t >= 1:
                    stB(t - 1)

        A.off = b_mark
        rS = [A.alloc((512,), F32) for _ in range(2)]
        tS = [A.alloc((512,), F32) for _ in range(2)]
        aS = A.alloc((512,), F32)
        sqS = A.alloc((512,), BF16)
        lnS = A.alloc((512,), F32)
        rsS = A.alloc((512,), F32)
        Vt = Vext.rearrange("p a b -> p (a b)")[:, 0:32 * 128].rearrange("p (a b) -> p a b", b=128)
        stg_n = [0]
        ptn = [0]
        da_cols = [[h * 128, 1024 + h * 128, 2048 + h * 128] for h in range(8)]
        dl_units = [(pr, g) for pr in range(4) for g in range(3)]
        dl_cols = [[3072 + g * 1536 + pr * 128, 3072 + g * 1536 + 512 + pr * 128, 3072 + g * 1536 + 1024 + pr * 128] for pr, g in dl_units]
        gate_cols = [[7680 + (gu + j) * 128 for j in range(min(3, 16 - gu))] for gu in range(0, 16, 3)]
        all_cols = da_cols + (dl_cols if stages >= 3 else []) + (gate_cols if stages >= 4 else [])
        pend = [load_w(all_cols[0])]
        ui = [0]

        def next_u():
            u = pend[0]
            ui[0] += 1
            if ui[0] < len(all_cols):
                pend[0] = load_w(all_cols[ui[0]])
            return u

        for h in range(8):
            u = next_u()
            proj_T_rope(u, 0, QT, "QT")
            proj_T_rope(u, 1, KT, "KT", split=KTz)
            for t in range(nt):
                b = pz[0] % 2
                pz[0] += 1
                for bb in range(4):
                    for k in range(8):
                        mm(pbank[b][:, bb * 128:(bb + 1) * 128], hT[:, k, (4 * t + bb) * 128:(4 * t + bb + 1) * 128],
                           wsl[u][2][:, k, :], k == 0, k == 7, [("w", u, 2), ("hT", t)], bk(b))
                cp("dve", Vt[:, 4 * t:4 * t + 4, :], pbank[b][:].rearrange("p (a c) -> p a c", c=128), bk(b),
                   [("V", 4 * t + i) for i in range(4)])
            steps = [(qt, c, kb) for qt in range(nt) for c in range(2) for kb in range(nb)]
            slots = {}

            def da_qk(i):
                qt, c, kb = steps[i]
                sb = 1 + i % 3
                mm(pbank[sb][:], KTz[c][:, kb * 128:(kb + 1) * 128], QT[:, qt * 512:(qt + 1) * 512],
                   True, True, [("KT", kb // 4, c), ("KTpad", c), ("QT", qt)], bk(sb))
                p = ptn[0] % 4
                ptn[0] += 1
                slots[i] = p
                act(PT[p], pbank[sb][:], AF.Exp, bk(sb), [("PT", p)], scale=0.125)

            def da_av(i):
                qt, c, kb = steps[i]
                p = slots.pop(i)
                mm(pbank[4 + c][:], Vt[:, kb, :], PT[p], kb == 0, kb == nb - 1, [("PT", p), ("V", kb)], bk(4 + c))
                mm(pbank[6 + c][:], ones_b, PT[p], kb == 0, kb == nb - 1, [("PT", p), "ones_b"], bk(6 + c))
                if kb == nb - 1:
                    recip(rS[c], pbank[6 + c][:], bk(6 + c), [("rS", c)])
                    tt("dve", tS[c], pbank[4 + c][:], rS[c], ALU.mult, bk(4 + c) + [("rS", c)], [("tS", c)])
                    if c == 1:
                        stt("dve", aS, tS[1], neglam, tS[0], ALU.mult, ALU.add, [("tS", 0), ("tS", 1), "neglam"], ["aS"])
                        tt("pool", sqS, aS, aS, ALU.mult, ["aS"], ["sqS"])
                        pending.append((i_now[0] + 10, qt))

            def da_norm2(qt):
                st = stage[stg_n[0] % 2]
                stn = ("stage", stg_n[0] % 2)
                stg_n[0] += 1
                mm(pbank[0][:], ones_b, sqS, True, True, ["ones_b", "sqS"], bk(0))
                act(lnS, pbank[0][:], AF.Ln, bk(0), ["lnS"], bias=epst, scale=1.0 / 128)
                act(rsS, lnS, AF.Exp, ["lnS"], ["rsS"], scale=-0.5)
                stt("dve", st, aS, gsubF, rsS, ALU.mult, ALU.mult, ["aS", "gsubF", "rsS"], [stn])
                P.dma("sp", oaT_scr[h, :, qt * 512:(qt + 1) * 512], st, f"st{stn[1]}", reads=[stn], writes=[("oaT_scr", h, qt)])

            n = len(steps)
            pending = []
            i_now = [0]
            for i in range(n + 2):
                i_now[0] = i
                if i < n:
                    da_qk(i)
                if i >= 2:
                    da_av(i - 2)
                while pending and pending[0][0] <= i:
                    da_norm2(pending.pop(0)[1])
            while pending:
                da_norm2(pending.pop(0)[1])

        if stages >= 3:
            P.barrier()
            A.off = b_mark
            acc = A.alloc((2, SMAX), F32)
            Osb = [A.alloc((256,), BF16) for _ in range(4)]
            for hd_ in range(2):
                mset("pool", Vd[:, :, hd_, 64:128], 1.0, [("V", i) for i in range(33)])
            pend_fin = []

            def emit_fin(pr_):
                for t in range(nt):
                    st = stage[stg_n[0] % 2]
                    stn = ("stage", stg_n[0] % 2)
                    stg_n[0] += 1
                    recip(acc[:, 1, t * 512:(t + 1) * 512], acc[:, 1, t * 512:(t + 1) * 512], [("acc", t)], [("acc", t)])
                    tt("dve", st, acc[:, 0, t * 512:(t + 1) * 512], acc[:, 1, t * 512:(t + 1) * 512], ALU.mult, [("acc", t)], [stn])
                    P.dma("sp", obT_scr[pr_, :, t * 512:(t + 1) * 512], st, f"st{stn[1]}", reads=[stn], writes=[("obT_scr", pr_, t)])

            for pr in range(4):
                for g, dil in enumerate((1, 4, 16)):
                    u = next_u()
                    if g not in DBG_GROUPS:
                        continue
                    proj_T_rope(u, 0, QT, "QT", perm=dil)
                    proj_T_rope(u, 1, KT, "KT", split=KTz, perm=dil)
                    while pend_fin:
                        emit_fin(pend_fin.pop(0))
                    Tn = S // dil
                    nj = Tn // 128
                    for t in range(nt):
                        b = pz[0] % 2
                        pz[0] += 1
                        for k in range(8):
                            mm(pbank[b][:], wsl[u][2][:, k, :], hT[:, k, t * 512:(t + 1) * 512], k == 0, k == 7,
                               [("w", u, 2), ("hT", t)], bk(b))
                        if dil == 1:
                            vdst = VTv[:, t * 512:(t + 1) * 512]
                            vsrc = pbank[b][:]
                        else:
                            w_ = 512 // dil
                            vdst = VTv[:, 0:S].rearrange("p (r t) -> p r t", r=dil)[:, :, t * w_:(t + 1) * w_]
                            vsrc = pbank[b][:].rearrange("p (a r) -> p r a", r=dil)
                        act(vdst, vsrc, AF.Copy, bk(b), rt_names)
                    blocks = []
                    for jj in range(nj + 1):
                        lo = max(128 * jj - 64, 0)
                        hi = min(128 * jj + 64, Tn)
                        blocks.append((lo, hi - lo))
                    def vbase(r_):
                        return 0 if dil == 1 else 16 * (r_ % 2)

                    def emit_vblocks(r_):
                        vb = vbase(r_)
                        for g4 in range(0, nj + 1, 4):
                            b = pz[0] % 2
                            pz[0] += 1
                            pvb = pbank[b][:].bitcast(BF16)
                            grp = blocks[g4:g4 + 4]
                            for bi, (lo, nk) in enumerate(grp):
                                c0 = r_ * Tn + lo
                                tr(pvb[0:nk, bi * 128:(bi + 1) * 128], VTv[:, c0:c0 + nk], ident, rt_names + ["ident"], bk(b))
                            ng = len(grp)
                            for hd_ in range(2):
                                cp("dve", Vd[:, vb + g4:vb + g4 + ng, hd_, 0:64],
                                   pvb[:, 0:ng * 128].rearrange("p (a h c) -> p a h c", h=2, c=64)[:, :, hd_, :],
                                   bk(b), [("V", vb + g4 + i) for i in range(ng)])

                    all_steps = [(r_, j_) for r_ in range(dil) for j_ in range(nj)]
                    nst = len(all_steps)
                    info = {}

                    def dl_qk(i):
                        r, j = all_steps[i]
                        sb = 2 + i % 2
                        q0 = r + dil * 128 * j
                        q1 = q0 + dil * 127 + 1
                        kinfo = []
                        mk = 1 if j == 0 else 0
                        mm(pbank[sb][:], ident, mask4[:, mk].rearrange("p a b -> p (a b)"), True, False, ["ident", "mask4"], bk(sb))
                        for ab in range(2):
                            lo, nk = blocks[j + ab]
                            kinfo.append(nk)
                            for hd in range(2):
                                qd = hd * 2 + ab
                                ksl = slice(r * Tn + lo, r * Tn + lo + nk)
                                qsl = slice(r * Tn + 128 * j, r * Tn + 128 * j + 128)
                                mm(pbank[sb][0:nk, qd * 128:(qd + 1) * 128], KTz[hd][:, ksl], QT[:, qsl], False, (ab == 1 and hd == 1),
                                   [("KT", t, hd) for t in range(nt)] + [("KTpad", hd)] + [("QT", t) for t in range(nt)], bk(sb))
                        p = ptn[0] % 4
                        ptn[0] += 1
                        act(PT[p], pbank[sb][:], AF.Exp, bk(sb), [("PT", p)], scale=0.125)
                        info[i] = (p, kinfo, q0, q1)

                    def dl_av(i):
                        r, j = all_steps[i]
                        p, kinfo, q0, q1 = info[i]
                        ob = 4 + i % 2
                        ops_ = pbank[ob][:, 0:256]
                        vb = vbase(r)
                        for hd in range(2):
                            for ab in range(2):
                                nk = kinfo[ab]
                                qd = hd * 2 + ab
                                mm(ops_[:, hd * 128:(hd + 1) * 128], PT[p][0:nk, qd * 128:(qd + 1) * 128], Vd[0:nk, vb + j + ab, hd, :],
                                   ab == 0, ab == 1, [("PT", p), ("V", vb + j + ab)], bk(ob))
                        cp("dve", Osb[i % 4].rearrange("p (a h c) -> p a h c", a=2, h=2),
                           ops_.rearrange("p (h a c) -> p a h c", h=2, a=2), bk(ob), [("Osb", i % 4)])

                    def dl_tr(i):
                        p, kinfo, q0, q1 = info.pop(i)
                        tb = 6 + i % 2
                        pv = pbank[tb][:].bitcast(BF16)
                        z = i % 4
                        for w_ in range(2):
                            tr(pv[:, w_ * 128:(w_ + 1) * 128], Osb[z][:, w_ * 128:(w_ + 1) * 128], ident,
                               [("Osb", z), "ident"], bk(tb))
                        names = [("acc", t) for t in range(q0 // 512, (q1 - 1) // 512 + 1)]
                        asl = slice(q0, q1) if dil == 1 else slice(q0, q1, dil)
                        src = pv[:, 0:256].rearrange("p (a c) -> p a c", c=128)
                        if g == 0:
                            cp("dve", acc[:, :, asl], src, bk(tb), names)
                        else:
                            tt("dve", acc[:, :, asl], acc[:, :, asl], src, ALU.add, bk(tb) + names, names)

                    emit_vblocks(0)
                    for i in range(nst + 3):
                        if i < nst:
                            dl_qk(i)
                        if 2 <= i < nst + 2:
                            dl_av(i - 2)
                        if 3 <= i:
                            dl_tr(i - 3)
                        if i >= 2 and (i - 2) % nj == 0:
                            rr = (i - 2) // nj + 1
                            if 1 <= rr < dil:
                                emit_vblocks(rr)
                pend_fin.append(pr)
            while pend_fin:
                emit_fin(pend_fin.pop(0))

        if stages >= 4:
            P.barrier()
            A.off = b_mark
            et = [A.alloc((512,), F32) for _ in range(2)]
            for gu in range(0, 16, 3):
                cols = [7680 + (gu + j) * 128 for j in range(min(3, 16 - gu))]
                u = next_u()
                for j in range(len(cols)):
                    for t in range(nt):
                        b = pz[0] % 2
                        pz[0] += 1
                        for k in range(8):
                            mm(pbank[b][:], wsl[u][j][:, k, :], hT[:, k, t * 512:(t + 1) * 512], k == 0, k == 7,
                               [("w", u, j), ("hT", t)], bk(b))
                        st = stage[stg_n[0] % 2]
                        stn = ("stage", stg_n[0] % 2)
                        stg_n[0] += 1
                        act(st, pbank[b][:], AF.Sigmoid, bk(b), [stn])
                        P.dma("sp", sg_scr[gu + j, :, t * 512:(t + 1) * 512], st, f"st{stn[1]}", reads=[stn], writes=[("sg_scr", gu + j, t)])
        P.barrier()
        if stages < 5:
            continue
        A.off = persist_mark
        wa = A.alloc((8, 1024), BF16)
        wb = A.alloc((4, 1024), BF16)
        wo = A.alloc((8, 1024), BF16)
        gt1bc = A.alloc((1024,), F32)
        diag = A.alloc((128,), F32)
        oin = [A.alloc((8, 512), BF16) for _ in range(2)]
        bin_ = [A.alloc((4, 512), BF16) for _ in range(2)]
        gin = [A.alloc((16, 512), BF16) for _ in range(2)]
        mT = A.alloc((8, 512), BF16)
        ct1 = [A.alloc((512,), F32) for _ in range(2)]
        ct2 = [A.alloc((512,), F32) for _ in range(2)]
        xin = [A.alloc((1024,), F32) for _ in range(4)]
        x1o = [A.alloc((1024,), F32) for _ in range(2)]
        P.dma("pool", wa, wa_d.rearrange("(k p) n -> p k n", p=128), "wa", writes=["wa"])
        P.dma("pool", wb, wb_d.rearrange("(k p) n -> p k n", p=128), "wb", writes=["wb"])
        P.dma("pool", wo, wo_d.rearrange("(k p) n -> p k n", p=128), "wo", writes=["wo"])
        prep_gt(s, 0, gt1bc, diag)
        oaT_v = oaT_scr.rearrange("h p s -> p h s")
        obT_v = obT_scr.rearrange("h p s -> p h s")
        sg_v = sg_scr.rearrange("h p s -> p h s")
        xc = [0]
        def c_loads(t):
            sl_ = t % 2
            P.dma("sp", oin[sl_], oaT_v[:, :, t * 512:(t + 1) * 512], f"oin{sl_}", writes=[("oin", sl_)])
            P.dma("sp", bin_[sl_], obT_v[:, :, t * 512:(t + 1) * 512], f"bin{sl_}", writes=[("bin", sl_)])
            P.dma("sp", gin[sl_], sg_v[:, :, t * 512:(t + 1) * 512], f"gin{sl_}", writes=[("gin", sl_)])

        c_loads(0)
        for t in range(nt):
            sl = t % 2
            if t + 1 < nt:
                c_loads(t + 1)
            for bb in range(4):
                r0_ = t0 + t * 512 + bb * 128
                P.dma("sp", xin[bb], x_d[r0_:r0_ + 128, :], f"xin{bb}", writes=[("xin", bb)])
            for m in range(8):
                ba = m % 2
                bb_ = 2 + m % 2
                for hh in range(8):
                    mm(pbank[ba][:], wa[:, hh, m * 128:(m + 1) * 128], oin[sl][:, hh, :], hh == 0, hh == 7, ["wa", ("oin", sl)], bk(ba))
                for pp in range(4):
                    mm(pbank[bb_][:], wb[:, pp, m * 128:(m + 1) * 128], bin_[sl][:, pp, :], pp == 0, pp == 3, ["wb", ("bin", sl)], bk(bb_))
                tt("dve", ct1[m % 2], pbank[ba][:], gin[sl][:, m, :], ALU.mult, bk(ba) + [("gin", sl)], [("ct1", m % 2)])
                tt("dve", ct2[m % 2], pbank[bb_][:], gin[sl][:, 8 + m, :], ALU.mult, bk(bb_) + [("gin", sl)], [("ct2", m % 2)])
                tt("pool", mT[:, m, :], ct1[m % 2], ct2[m % 2], ALU.add, [("ct1", m % 2), ("ct2", m % 2)], [("mT", m)])
            for bb in range(4):
                xs = bb
                xo = xc[0] % 2
                xc[0] += 1
                r0 = t0 + t * 512 + bb * 128
                for half in range(2):
                    py = 4 + (2 * bb + half) % 4
                    for m in range(8):
                        mm(pbank[py][:], mT[:, m, bb * 128:(bb + 1) * 128], wo[:, m, half * 512:(half + 1) * 512], m == 0, m == 7,
                           [("mT", m), "wo"], bk(py))
                    tt("dve", ct1[half], pbank[py][:], gt1bc[:, half * 512:(half + 1) * 512], ALU.mult, bk(py) + ["gtbc"], [("ct1", half)])
                    tt("pool", x1o[xo][:, half * 512:(half + 1) * 512], ct1[half], xin[xs][:, half * 512:(half + 1) * 512], ALU.add,
                       [("ct1", half), ("xin", xs)], [("x1o", xo)])
                P.dma("sp", x1_scr[r0:r0 + 128, :], x1o[xo], f"x1o{xo}", reads=[("x1o", xo)], writes=[("x1_scr", r0)])
        P.barrier()

    if stages >= 6:
        A.off = persist_mark
        wgu = A.alloc((8, 2 * FF), BF16)
        wdn = A.alloc((22, 1024), BF16)
        gt2bc = A.alloc((1024,), F32)
        gfin = A.alloc((1024,), F32)
        h2T = A.alloc((8, 512), BF16)
        actT = A.alloc((22, 512), BF16)
        x1in = [A.alloc((1024,), F32) for _ in range(4)]
        hn2 = [A.alloc((1024,), BF16)] * 2
        junk = hn2[0]
        tmpA = A.alloc((8, 128), F32)
        diag = tmpA[:, 0, :]
        et = [A.alloc((512,), F32) for _ in range(2)]
        x2 = [A.alloc((1024,), F32)] * 2
        wgu_v = wgu_d.rearrange("(k p) n -> p k n", p=128)
        wdn_v = wdn_d.rearrange("(k p) n -> p k n", p=128)
        for c in range(11):
            P.dma("pool", wgu[:, :, c * 512:(c + 1) * 512], wgu_v[:, :, c * 512:(c + 1) * 512], f"wgu{c}", writes=[("wgu", c)])
        P.dma("pool", wdn[:, 0:11, :], wdn_v[:, 0:11, :], "wdn0", writes=[("wdn", 0)])
        P.dma("pool", wdn[:, 11:22, :], wdn_v[:, 11:22, :], "wdn1", writes=[("wdn", 1)])
        ld(gfin, gfin_d.partition_broadcast(128), "gfin", "gfin")
        wgu_all = [("wgu", c) for c in range(11)]
        x1n = [x1in[0], x1in[1]]
        xr = [x1in[2], x1in[3], x2[0]]
        tiles = [(s_, t_) for s_ in range(nseq) for t_ in range(seq_lens[s_] // 512)]
        nctr = [0]
        rctr = [0]
        pend_nb = {}

        nslot = {}

        def normA_load(idx, bb):
            s_, t_ = tiles[idx]
            xs = nctr[0] % 2
            nctr[0] += 1
            nslot[(idx, bb)] = xs
            r0_ = seq_off[s_] + t_ * 512 + bb * 128
            P.dma("sp", x1n[xs], x1_scr[r0_:r0_ + 128, :], f"x1n{xs}", writes=[("x1n", xs)])

        def normA_blk(idx, bb):
            s_, t_ = tiles[idx]
            if t_ == 0 and bb == 0:
                prep_m(s_, 1)
            if (idx, bb) not in nslot:
                normA_load(idx, bb)
            xs = nslot.pop((idx, bb))
            norm_A(x1n[xs], ("x1n", xs), hn2[0], ("hn2", 0), 4 * (bb % 2), junk, jname=("hn2", 0))

        def normB_blk(idx, bb):
            s_, t_ = tiles[idx]
            norm_B(hn2[0], ("hn2", 0), tmpA, h2T, [("h2T", bb)], 1, s_, bb * 128, bb % 2)

        for bb in range(4):
            normA_blk(0, bb)
            normB_blk(0, bb)
        for idx, (s, t) in enumerate(tiles):
            t0 = seq_off[s]
            if t == 0:
                prep_gt(s, 1, gt2bc, diag, "tmpA")
            h2names = [("h2T", bb) for bb in range(4)]
            for f in range(22):
                bg = 2 + f % 2
                bu = 4 + f % 2
                for k in range(8):
                    mm(pbank[bg][:], wgu[:, k, f * 128:(f + 1) * 128], h2T[:, k, :], k == 0, k == 7, wgu_all + h2names, bk(bg))
                for k in range(8):
                    mm(pbank[bu][:], wgu[:, k, FF + f * 128:FF + (f + 1) * 128], h2T[:, k, :], k == 0, k == 7, wgu_all + h2names, bk(bu))
                e_ = et[f % 2]
                en = ("et", f % 2)
                act(e_, pbank[bg][:], AF.Silu, bk(bg), [en])
                tt("dve", actT[:, f, :], e_, pbank[bu][:], ALU.mult, [en] + bk(bu), [("actT", f)])
            anames = [("actT", f) for f in range(22)]
            nxt = idx + 1 < len(tiles)
            rslot = {}

            def issue_loads(bb_):
                xo_ = rctr[0] % 3
                rctr[0] += 1
                rslot[bb_] = xo_
                r0_ = t0 + t * 512 + bb_ * 128
                P.dma("sp", xr[xo_], x1_scr[r0_:r0_ + 128, :], f"xr{xo_}", writes=[("xr", xo_)])
                if nxt:
                    normA_load(idx + 1, bb_)

            issue_loads(0)
            for bb in range(4):
                if bb + 1 < 4:
                    issue_loads(bb + 1)
                xo = rslot[bb]
                r0 = t0 + t * 512 + bb * 128
                if nxt:
                    normA_blk(idx + 1, bb)
                for half in range(2):
                    py = 6 + half
                    for f in range(22):
                        mm(pbank[py][:], actT[:, f, bb * 128:(bb + 1) * 128], wdn[:, f, half * 512:(half + 1) * 512], f == 0, f == 21,
                           anames + [("wdn", 0), ("wdn", 1)], bk(py))
                    tt("dve", et[half], pbank[py][:], gt2bc[:, half * 512:(half + 1) * 512], ALU.mult, bk(py) + ["gtbc"], [("et", half)])
                    tt("pool", xr[xo][:, half * 512:(half + 1) * 512], et[half], xr[xo][:, half * 512:(half + 1) * 512], ALU.add,
                       [("et", half), ("xr", xo)], [("xr", xo)])
                if nxt:
                    normB_blk(idx + 1, bb)
                so = 16 + 4 * xo
                ss = small[:, so:so + 1]
                P.op("act", lambda e, a=ss: e.memzero(a), (), [("small", so)])
                act(junk, xr[xo], AF.Square, [("xr", xo), ("small", so)], [("hn2", 0), ("small", so)], accum=ss)
                act(small[:, so + 1:so + 2], ss, AF.Ln, [("small", so)], [("small", so + 1)], bias=epst, scale=1.0 / D)
                act(small[:, so + 2:so + 3], small[:, so + 1:so + 2], AF.Exp, [("small", so + 1)], [("small", so + 2)], scale=-0.5)
                stt("dve", xr[xo], xr[xo], small[:, so + 2:so + 3], gfin, ALU.mult, ALU.mult,
                    [("xr", xo), ("small", so + 2), "gfin"], [("xr", xo)])
                P.dma("sp", y_d[r0:r0 + 128, :], xr[xo], f"yo{xo}", reads=[("xr", xo)])
        final_keys = ["yo0", "yo1", "yo2"]
    else:
        final_keys = []
    print("ops:", {e: len(P.streams[e]) for e in ENGS}, "arena", A.off)
    with nc.allow_low_precision(reason="bf16 matmul operands by design"):
        P.emit(final_wait_keys=final_keys)
    return nc


def make_consts():
    bf = ml_dtypes.bfloat16
    ident = np.eye(128, dtype=np.float32)
    rmatT = np.zeros((128, 128), np.float32)
    for b in (0, 64):
        for i in range(8):
            rmatT[b + i + 8, b + i] = -1.0
            rmatT[b + i, b + i + 8] = 1.0
    kk = np.arange(128)[:, None]
    qq = np.arange(128)[None, :]
    mA = (kk >= qq).astype(np.float32)
    mB = (kk <= qq).astype(np.float32)
    mA0 = ((kk + 64 >= qq) & (kk < 64)).astype(np.float32)
    m_reg = np.stack([mA, mB, mA, mB], axis=1)
    m_first = np.stack([mA0, mB, mA0, mB], axis=1)
    mask4 = (np.stack([m_reg, m_first], axis=1).reshape(128, 1024) - 1.0) * 30000.0
    inv = (np.float32(1.0) / (np.float32(500000.0) ** (np.arange(0, 16, 2, dtype=np.float32) / np.float32(16)))).astype(np.float32)
    ang = (np.arange(4096, dtype=np.float32)[:, None] * inv[None, :]).astype(np.float32)
    cos = np.cos(ang).astype(np.float32)
    sin = np.sin(ang).astype(np.float32)
    ropeC = np.ones((128, 4096), np.float32)
    ropeS = np.zeros((128, 4096), np.float32)
    for b in (0, 64):
        for i in range(8):
            ropeC[b + i] = cos[:, i]
            ropeC[b + i + 8] = cos[:, i]
            ropeS[b + i] = sin[:, i]
            ropeS[b + i + 8] = sin[:, i]
    return {
        "ident": ident.astype(bf), "identf": ident, "rmatT": rmatT.astype(bf), "mask4": mask4.astype(bf),
        "ropeC": ropeC, "ropeS": ropeS,
    }


def make_in_map(xs, cs, w, consts):
    m = dict(consts)
    m["x"] = np.ascontiguousarray(np.concatenate(xs, axis=0), dtype=np.float32)
    cT = np.stack(cs, axis=0).reshape(len(cs), 8, 128).transpose(2, 1, 0)
    m["cT"] = np.ascontiguousarray(cT.reshape(128, 8 * len(cs)), dtype=np.float32)
    m["w_ada"] = w["w_ada"][0]
    m["b_adaF"] = np.ascontiguousarray(w["b_ada"][0].reshape(48, 128).T)
    m["g_mixF"] = np.ascontiguousarray(w["g_mix"][0].reshape(8, 128).T)
    m["g_ffnF"] = np.ascontiguousarray(w["g_ffn"][0].reshape(8, 128).T)
    m["w_in"] = w["w_in"][0]
    for n in ("lam_q1", "lam_k1", "lam_q2", "lam_k2", "g_subln", "w_a", "w_b", "w_out", "w_gu", "w_down"):
        m[n] = w[n][0]
    m["g_final"] = w["g_final"]
    return {k: np.ascontiguousarray(v) for k, v in m.items()}


def kernel(**inputs):
    inp = {k: np.asarray(v) for k, v in inputs.items()}
    xp, xs_ = inp["x_prompt"], inp["x_sample"]
    cp_, cs_ = inp["c_prompt"], inp["c_sample"]
    n = 8
    seq_lens = [xp.shape[1], xp.shape[1], xs_.shape[1], xs_.shape[1]]
    nc = build_program(seq_lens)
    consts = make_consts()
    in_maps = []
    for c in range(n):
        xs = [xp[2 * c], xp[2 * c + 1], xs_[2 * c], xs_[2 * c + 1]]
        cs = [cp_[2 * c], cp_[2 * c + 1], cs_[2 * c], cs_[2 * c + 1]]
        in_maps.append(make_in_map(xs, cs, inp, consts))
    res = run_bass_kernel_spmd(nc, in_maps, core_ids=list(range(n)))
    yp = np.empty_like(xp, dtype=np.float32)
    ys = np.empty_like(xs_, dtype=np.float32)
    Sp, Ss = xp.shape[1], xs_.shape[1]
    for c in range(n):
        y = res.results[c]["y"]
        yp[2 * c] = y[0:Sp]
        yp[2 * c + 1] = y[Sp:2 * Sp]
        ys[2 * c] = y[2 * Sp:2 * Sp + Ss]
        ys[2 * c + 1] = y[2 * Sp + Ss:2 * Sp + 2 * Ss]
    return (yp, ys)
```
